# Optimizing a Trainium2 kernel written in Bass

```python
import jax, jax.numpy as jnp
from jax import lax
import numpy as np

D_MODEL = 1024
BATCH = 8
SEQ = 4096
DEPTH = 4

N_A = DEPTH // 2
N_B = DEPTH - N_A
PLE_DIM = 256
NORM_EPS = 1e-6
RW_HEAD = 64
RW_WIDTH = D_MODEL
RW_HEADS = RW_WIDTH // RW_HEAD
DECAY_LORA = 64
AAA_LORA = 64
MV_LORA = 32
GN_EPS = 64e-5
ATT_HEAD = 128
KV_HEADS = 8
ATT_WIDTH = KV_HEADS * ATT_HEAD
DILATION_GROUPS = ((128, 1), (512, 4), (2048, 16))
N_GROUPS = len(DILATION_GROUPS)
Q_WIDTH = N_GROUPS * ATT_WIDTH
BLOCK = 128

kernel_name = "yoco_rwkv7_dilated_alibi_hybrid"


def rmsnorm(x, g):
    xf = x.astype(jnp.float32)
    y = xf * lax.rsqrt(jnp.mean(xf * xf, axis=-1, keepdims=True) + NORM_EPS)
    return (y * g.astype(jnp.float32)).astype(x.dtype)


def token_shift(x):
    return jnp.pad(x, ((0, 0), (1, 0), (0, 0)))[:, :-1]


def wkv7(r, decay, k, v, kk, a):
    B, S, H, N = r.shape

    def step(state, inp):
        r_t, w_t, k_t, v_t, kk_t, a_t = inp
        sa = jnp.einsum('bhij,bhj->bhi', state, -kk_t)
        state = (state * w_t[:, :, None, :]
                 + sa[..., None] * (kk_t * a_t)[:, :, None, :]
                 + v_t[..., None] * k_t[:, :, None, :])
        return state, jnp.einsum('bhij,bhj->bhi', state, r_t)

    seq = tuple(jnp.moveaxis(t, 1, 0) for t in (r, decay, k, v, kk, a))
    s0 = jnp.zeros((B, H, N, N), jnp.float32)
    _, y = lax.scan(step, s0, seq)
    return jnp.moveaxis(y, 0, 1)


def rwkv7_time_mix(x, v_first, mu, w_rkvg, w0, w1, w2, a0, a1, a2, vres,
                   k_k, k_a, r_k, gn_g, gn_b, w_o):
    f32 = jnp.float32
    B, S, _ = x.shape
    xx = token_shift(x) - x
    xm = x[None] + xx[None] * mu[:, None, None, :]
    r, k, v, gate = jnp.einsum('nbsd,nde->nbse', xm[:4], w_rkvg)
    w_log = -jax.nn.softplus(-(w0 + jnp.tanh(xm[4] @ w1) @ w2).astype(f32)) - 0.5
    decay = jnp.exp(-jnp.exp(w_log))
    a = jax.nn.sigmoid((a0 + (xm[5] @ a1) @ a2).astype(f32))
    if vres is not None:
        v0, v1, v2 = vres
        v = v + (v_first - v) * jax.nn.sigmoid(v0 + (xm[2] @ v1) @ v2)

    def heads(t):
        return t.astype(f32).reshape(B, S, RW_HEADS, RW_HEAD)

    kk = heads(k * k_k)
    kk = kk / jnp.maximum(jnp.sqrt(jnp.sum(kk * kk, axis=-1, keepdims=True)), 1e-12)
    k = k * (1.0 + (a - 1.0) * k_a)
    rh, kh, vh = heads(r), heads(k), heads(v)
    y = wkv7(rh, heads(decay), kh, vh, kk, heads(a))
    mean = jnp.mean(y, axis=-1, keepdims=True)
    yc = y - mean
    y = yc * lax.rsqrt(jnp.mean(yc * yc, axis=-1, keepdims=True) + GN_EPS)
    y = y.reshape(B, S, RW_WIDTH) * gn_g + gn_b
    y = y + (jnp.sum(rh * kh * r_k, axis=-1, keepdims=True) * vh).reshape(B, S, RW_WIDTH)
    out = (y * jax.nn.silu(gate.astype(f32))) @ w_o
    return out.astype(x.dtype), v


def to_strided_blocks(t, d):
    B, S, H, E = t.shape
    n = S // d
    nb = -(-n // BLOCK)
    t = t.reshape(B, n, d, H, E).transpose(0, 2, 1, 3, 4)
    t = jnp.pad(t, ((0, 0), (0, 0), (0, nb * BLOCK - n), (0, 0), (0, 0)))
    return t.reshape(B, d, nb, BLOCK, H, E)


def from_strided_blocks(t, S):
    B, d, nb, L = t.shape[:4]
    rest = t.shape[4:]
    n = S // d
    t = t.reshape((B, d, nb * L) + rest)[:, :, :n]
    t = jnp.moveaxis(t, 1, 2)
    return t.reshape((B, S) + rest)


def with_prev_block(t):
    prev = jnp.pad(t, ((0, 0), (0, 0), (1, 0), (0, 0), (0, 0), (0, 0)))[:, :, :-1]
    return jnp.concatenate([prev, t], axis=3)


def alibi_slopes(d):
    h = np.arange(1, KV_HEADS + 1, dtype=np.float32)
    return jnp.asarray((2.0 ** (-8.0 * h / KV_HEADS)) / d, dtype=jnp.float32)


def shared_kv_windows(h, kv_ln_g, w_kv):
    B, S, _ = h.shape
    kv = rmsnorm(h, kv_ln_g) @ w_kv
    k = kv[..., :ATT_WIDTH].reshape(B, S, KV_HEADS, ATT_HEAD)
    v = kv[..., ATT_WIDTH:].reshape(B, S, KV_HEADS, ATT_HEAD)
    return tuple((with_prev_block(to_strided_blocks(k, d)), with_prev_block(to_strided_blocks(v, d)))
                 for (_, d) in DILATION_GROUPS)


def dilated_group_attention(q, kw, vw, d, span, slopes):
    S = q.shape[1]
    qb = to_strided_blocks(q, d)
    nb = qb.shape[2]
    s = jnp.einsum('brcihe,brcjhe->brchij', qb, kw).astype(jnp.float32) * (ATT_HEAD ** -0.5)
    i = jnp.arange(BLOCK)[:, None]
    j = jnp.arange(2 * BLOCK)[None, :]
    delta = BLOCK + i - j
    c = jnp.arange(nb)[:, None, None]
    valid = (delta >= 0) & (delta <= span) & (c * BLOCK + j - BLOCK >= 0)
    bias = -slopes[:, None, None] * (delta * d).astype(jnp.float32)[None]
    s = jnp.where(valid[:, None], s + bias, -jnp.inf)
    m = jnp.max(s, axis=-1, keepdims=True)
    pexp = jnp.exp(s - m)
    l = jnp.sum(pexp, axis=-1, keepdims=True)
    o = jnp.einsum('brchij,brcjhe->brcihe', pexp / l, vw)
    lse = jnp.moveaxis((m + jnp.log(l))[..., 0], -1, -2)
    return from_strided_blocks(o, S), from_strided_blocks(lse, S)


def dilated_attention_mix(x, kv_windows, w_in, w_o):
    B, S, _ = x.shape
    proj = x @ w_in
    q = proj[..., :Q_WIDTH].reshape(B, S, N_GROUPS, KV_HEADS, ATT_HEAD)
    gate = proj[..., Q_WIDTH:]
    outs, lses = [], []
    for g, (window, d) in enumerate(DILATION_GROUPS):
        kw, vw = kv_windows[g]
        o, lse = dilated_group_attention(q[:, :, g], kw, vw, d, window // d, alibi_slopes(d))
        outs.append(o)
        lses.append(lse)
    alpha = jax.nn.softmax(jnp.stack(lses), axis=0)
    o = jnp.einsum('gbsh,gbshe->bshe', alpha, jnp.stack(outs)).reshape(B, S, ATT_WIDTH)
    return ((o * jax.nn.silu(gate.astype(jnp.float32))) @ w_o).astype(x.dtype)


def per_layer_embedding(h, p_i, w_p, gate_ln_g, w_gate):
    gate = jax.nn.sigmoid((rmsnorm(h, gate_ln_g) @ w_gate).astype(jnp.float32))
    return ((p_i @ w_p) * gate).astype(h.dtype)


def setup_inputs(seed: int = 0) -> dict:
    key = jax.random.key(seed)
    ks = iter(jax.random.split(key, 40))
    D = D_MODEL
    na1 = N_A - 1

    def nrm(shape, scale):
        return scale * jax.random.normal(next(ks), shape, jnp.float32)

    def gain(shape):
        return 1.0 + nrm(shape, 0.02)

    return {
        "x": nrm((BATCH, SEQ, D), 1.0),
        "p": nrm((DEPTH, BATCH, SEQ, PLE_DIM), 1.0),
        "a_ln_g": gain((N_A, D)),
        "a_mu": jax.random.uniform(next(ks), (N_A, 6, D), jnp.float32),
        "a_w_rkvg": nrm((N_A, 4, D, RW_WIDTH), D ** -0.5),
        "a_w0": jax.random.uniform(next(ks), (N_A, RW_WIDTH), jnp.float32, -6.0, 1.0),
        "a_w1": nrm((N_A, D, DECAY_LORA), D ** -0.5),
        "a_w2": nrm((N_A, DECAY_LORA, RW_WIDTH), 0.3 * DECAY_LORA ** -0.5),
        "a_a0": nrm((N_A, RW_WIDTH), 0.5),
        "a_a1": nrm((N_A, D, AAA_LORA), D ** -0.5),
        "a_a2": nrm((N_A, AAA_LORA, RW_WIDTH), 0.5 * AAA_LORA ** -0.5),
        "a_v0": 1.0 + nrm((na1, RW_WIDTH), 0.1),
        "a_v1": nrm((na1, D, MV_LORA), D ** -0.5),
        "a_v2": nrm((na1, MV_LORA, RW_WIDTH), 0.5 * MV_LORA ** -0.5),
        "a_k_k": 0.85 + nrm((N_A, RW_WIDTH), 0.05),
        "a_k_a": 1.0 + nrm((N_A, RW_WIDTH), 0.05),
        "a_r_k": nrm((N_A, RW_HEADS, RW_HEAD), 0.1),
        "a_gn_g": gain((N_A, RW_WIDTH)),
        "a_gn_b": nrm((N_A, RW_WIDTH), 0.02),
        "a_w_o": nrm((N_A, RW_WIDTH, D), 0.5 * RW_WIDTH ** -0.5),
        "kv_ln_g": gain((D,)),
        "w_kv": nrm((D, 2 * ATT_WIDTH), D ** -0.5),
        "b_ln_g": gain((N_B, D)),
        "b_w_in": nrm((N_B, D, Q_WIDTH + ATT_WIDTH), D ** -0.5),
        "b_w_o": nrm((N_B, ATT_WIDTH, D), 0.5 * ATT_WIDTH ** -0.5),
        "ple_w": nrm((DEPTH, PLE_DIM, D), 0.5 * PLE_DIM ** -0.5),
        "ple_gate_ln_g": gain((DEPTH, D)),
        "ple_w_gate": nrm((DEPTH, D, D), D ** -0.5),
        "final_ln_g": gain((D,)),
    }


def reference(x, p, a_ln_g, a_mu, a_w_rkvg, a_w0, a_w1, a_w2, a_a0, a_a1, a_a2,
              a_v0, a_v1, a_v2, a_k_k, a_k_a, a_r_k, a_gn_g, a_gn_b, a_w_o,
              kv_ln_g, w_kv, b_ln_g, b_w_in, b_w_o,
              ple_w, ple_gate_ln_g, ple_w_gate, final_ln_g):
    h = x
    v_first = None
    kv_windows = None
    for i in range(DEPTH):
        if i < N_A:
            vres = None if i == 0 else (a_v0[i - 1], a_v1[i - 1], a_v2[i - 1])
            mix, v = rwkv7_time_mix(rmsnorm(h, a_ln_g[i]), v_first, a_mu[i], a_w_rkvg[i],
                                    a_w0[i], a_w1[i], a_w2[i], a_a0[i], a_a1[i], a_a2[i], vres,
                                    a_k_k[i], a_k_a[i], a_r_k[i], a_gn_g[i], a_gn_b[i], a_w_o[i])
            if i == 0:
                v_first = v
        else:
            j = i - N_A
            mix = dilated_attention_mix(rmsnorm(h, b_ln_g[j]), kv_windows, b_w_in[j], b_w_o[j])
        h = h + mix
        h = h + per_layer_embedding(h, p[i], ple_w[i], ple_gate_ln_g[i], ple_w_gate[i])
        if i == N_A - 1:
            kv_windows = shared_kv_windows(h, kv_ln_g, w_kv)
    return rmsnorm(h, final_ln_g)
```

```python
import numpy as np
import concourse.bass as bass
import concourse.mybir as mybir
from concourse.bass_utils import run_bass_kernel_spmd

F32 = mybir.dt.float32
BF16 = mybir.dt.bfloat16
ALU = mybir.AluOpType
AF = mybir.ActivationFunctionType

S_LEN = 4096
D = 1024
T = 128
NT = S_LEN // T
C0 = 0.6065306597126334
N_DMA_SEM = 8
SAME_ENGINE_SYNC = True

VEC_NAMES = []
for _i in range(2):
    VEC_NAMES += ["a_ln_g%d" % _i] + ["a_mu%d_%d" % (_i, n) for n in range(6)]
    VEC_NAMES += [s + str(_i) for s in ("a_w0", "a_a0", "a_k_k", "a_k_a", "a_r_k", "a_gn_g", "a_gn_b", "b_ln_g")]
VEC_NAMES += ["a_v0", "kv_ln_g", "final_ln_g"] + ["ple_g%d" % i for i in range(4)]
VIDX = {n: i for i, n in enumerate(VEC_NAMES)}
NVEC = len(VEC_NAMES)


class Buf:
    __slots__ = ("name", "last_write", "readers", "excl")

    def __init__(self, name=""):
        self.name = name
        self.last_write = None
        self.readers = {}
        self.excl = False


class TT_:
    def __init__(self, h, name=""):
        self.h = h
        self.b = Buf(name)
        self.b_tw = 0.0
        self.b_tr = 0.0

    def __getitem__(self, k):
        return self.h[k]


class Sch:
    def __init__(self, nc):
        self.nc = nc
        self.engs = {"pe": nc.tensor, "act": nc.scalar, "dve": nc.vector,
                     "pool": nc.gpsimd, "sp": nc.sync}
        self.sems = {}
        self.cnt = {}
        self.ops = {e: [] for e in self.engs}
        self.seen = {e: {} for e in self.engs}
        for e in self.engs:
            self.sems[e] = nc.alloc_semaphore("s_" + e)
            self.cnt[e] = 0
        self.dma_uses = {}
        self.dma_rr = {}
        for q in ("sp", "pool", "act"):
            for i in range(N_DMA_SEM):
                key = "d_%s%d" % (q, i)
                self.sems[key] = nc.alloc_semaphore(key)
                self.dma_uses[key] = 0
            self.dma_rr[q] = 0

    def _need(self, eng, waits, ev):
        if ev is None:
            return
        key, val, src = ev
        if src == eng and (eng in ("pe", "sp") or (not SAME_ENGINE_SYNC and eng != "pool")):
            return
        if self.seen[eng].get(key, 0) >= val:
            return
        if waits.get(key, 0) < val:
            waits[key] = val

    def _deps(self, eng, reads, writes):
        waits = {}
        for b in reads:
            self._need(eng, waits, b.last_write)
        for b in writes:
            self._need(eng, waits, b.last_write)
            for key, (val, src) in b.readers.items():
                self._need(eng, waits, (key, val, src))
        for key, val in waits.items():
            self.seen[eng][key] = val
        return waits

    @staticmethod
    def _mark(ev, reads, writes):
        key, val, src = ev
        for b in reads:
            b.readers[key] = (val, src)
        for b in writes:
            b.last_write = ev
            b.readers = {}

    def op(self, eng, fn, reads=(), writes=()):
        writes = [x.b for x in writes] + [x.b for x in reads if x.b.excl]
        reads = [x.b for x in reads if not x.b.excl]
        waits = self._deps(eng, reads, writes)
        self.cnt[eng] += 1
        ev = (eng, self.cnt[eng], eng)
        self._mark(ev, reads, writes)
        self.ops[eng].append((list(waits.items()), fn, eng, 1))

    def dma(self, q, fn, reads=(), writes=()):
        reads = [x.b for x in reads]
        writes = [x.b for x in writes]
        i = self.dma_rr[q]
        self.dma_rr[q] = (i + 1) % N_DMA_SEM
        key = "d_%s%d" % (q, i)
        waits = self._deps(q, reads, writes)
        prev = 16 * self.dma_uses[key]
        if prev and self.seen[q].get(key, 0) < prev:
            waits[key] = max(waits.get(key, 0), prev)
            self.seen[q][key] = prev
        self.dma_uses[key] += 1
        ev = (key, 16 * self.dma_uses[key], "dma")
        self._mark(ev, reads, writes)
        self.ops[q].append((list(waits.items()), fn, key, 16))

    def barrier(self):
        targets = {}
        for e in self.engs:
            if self.cnt[e]:
                targets[e] = self.cnt[e]
        for key, n in self.dma_uses.items():
            if n:
                targets[key] = 16 * n
        for e in self.engs:
            waits = []
            for key, val in targets.items():
                if key == e and (e in ("pe", "sp") or not SAME_ENGINE_SYNC):
                    continue
                if self.seen[e].get(key, 0) < val:
                    waits.append((key, val))
                    self.seen[e][key] = val
            if waits:
                self.ops[e].append((waits, None, None, 0))

    def emit(self):
        nc = self.nc
        sems = self.sems

        def run(name):
            def body(eng):
                for waits, fn, inc_key, inc in self.ops[name]:
                    for key, val in waits:
                        eng.wait_ge(sems[key], val)
                    if fn is not None:
                        ins = fn(eng)
                        ins.then_inc(sems[inc_key], inc)
            return body

        with nc.Block() as block:
            block.tensor(run("pe"))
            block.scalar(run("act"))
            block.vector(run("dve"))
            block.gpsimd(run("pool"))
            block.sync(run("sp"))


class K:
    def __init__(self, nc):
        self.nc = nc
        self.s = Sch(nc)
        self.sb_base = nc.sbuf_base + 64
        self.sb_base += (-self.sb_base) % 64
        self.sb_off = self.sb_base
        self.sb_top = nc.sbuf_top
        self.uid = 0
        self.psu = [TT_(nc.alloc_psum_tensor("psu%d" % i, [128, 512], F32), "psu%d" % i) for i in range(8)]
        for p_ in self.psu:
            p_.b.excl = True
        self.ps_units = tuple(range(8))
        self.ps_rrs = {}
        self.rr = 0
        self.rec = None
        self.t_avail = {}
        self.last_n = 0

    def sb(self, shape, dt, name="t"):
        nbytes = int(np.prod(shape[1:])) * (4 if dt == F32 else 2)
        nbytes += (-nbytes) % 64
        assert self.sb_off + nbytes <= self.sb_top, ("SBUF overflow", name, self.sb_off, nbytes)
        self.uid += 1
        h = self.nc.alloc_sbuf_tensor_at("%s_%d" % (name, self.uid), list(shape), dt, offset=self.sb_off)
        self.sb_off += nbytes
        return TT_(h, name)

    def mark(self):
        return self.sb_off

    def release(self, m):
        self.s.barrier()
        self.sb_off = m

    def ps(self):
        units = self.ps_units
        i = self.ps_rrs.get(units, 0)
        self.ps_rrs[units] = (i + 1) % len(units)
        return self.psu[units[i]]

    def alt(self, engs=("dve", "pool")):
        self.rr += 1
        return engs[self.rr % len(engs)]

    def _op(self, eng, fn, R, W):
        if self.rec is not None:
            n = self.last_n if eng != "dve_recip" else 6 * self.last_n
            self.rec.append((0, eng, fn, list(R), list(W), n))
        else:
            self.s.op(eng, fn, R, W)

    def _dma(self, q, fn, R, W):
        if self.rec is not None:
            self.rec.append((1, q, fn, list(R), list(W), 0))
        else:
            self.s.dma(q, fn, R, W)

    def record(self, fn, units):
        assert self.rec is None
        self.rec = []
        old = self.ps_units
        self.ps_units = tuple(units)
        try:
            fn()
            return self.rec
        finally:
            self.rec = None
            self.ps_units = old

    def _cost(self, kind, eng, n):
        if kind == 1:
            return 0.1, 3.0
        if eng == "pe":
            d = 0.035 + 0.00045 * n
        elif eng == "act":
            d = 0.2 + 0.00095 * n
        elif eng == "dve":
            d = 0.1 + 0.0011 * n
        else:
            d = 0.2 + 0.0022 * n
        return d, d

    def merge(self, streams):
        pos = [0] * len(streams)
        total = sum(len(x) for x in streams)
        avail = self.t_avail
        for _ in range(total):
            best, bt, bfin = -1, None, None
            for i, st in enumerate(streams):
                if pos[i] >= len(st):
                    continue
                kind, e, fn, R, W, n = st[pos[i]]
                t0 = avail.get(e, 0.0)
                for x in R:
                    t0 = max(t0, x.b_tw)
                for x in W:
                    t0 = max(t0, x.b_tw, x.b_tr)
                key = (t0, pos[i] / len(st))
                if bt is None or key < bt:
                    best, bt = i, key
            kind, e, fn, R, W, n = streams[best][pos[best]]
            pos[best] += 1
            t0 = bt[0]
            busy, lat = self._cost(kind, e, n)
            avail[e] = t0 + busy
            fin = t0 + lat
            for x in R:
                x.b_tr = max(x.b_tr, fin)
            for x in W:
                x.b_tw = fin
                x.b_tr = 0.0
            if kind == 0:
                self.s.op(e, fn, R, W)
            else:
                self.s.dma(e, fn, R, W)

    def tt(self, eng, out, a, b, op, R, W):
        self.last_n = int(np.prod(out.shape[1:]))
        self._op(eng, lambda e: e.tensor_tensor(out, a, b, op), R, W)

    def stt(self, eng, out, in0, scalar, in1, op0, op1, R, W):
        self.last_n = int(np.prod(out.shape[1:]))
        self._op(eng, lambda e: e.scalar_tensor_tensor(out, in0, scalar, in1, op0, op1), R, W)

    def ts(self, eng, out, in0, s1, s2, op0, op1, R, W):
        self.last_n = int(np.prod(out.shape[1:]))
        if op1 is None:
            self._op(eng, lambda e: e.tensor_scalar(out, in0, s1, None, op0), R, W)
        else:
            self._op(eng, lambda e: e.tensor_scalar(out, in0, s1, s2, op0, op1), R, W)

    def act(self, out, in_, func, R, W, bias=None, scale=1.0):
        self.last_n = int(np.prod(out.shape[1:]))
        if bias is None:
            self._op("act", lambda e: e.activation(out, in_, func, scale=scale), R, W)
        else:
            self._op("act", lambda e: e.activation(out, in_, func, bias=bias, scale=scale), R, W)

    def cp(self, eng, out, in_, R, W):
        self.last_n = int(np.prod(out.shape[1:]))
        if eng == "act":
            self._op("act", lambda e: e.copy(out, in_), R, W)
        else:
            self._op(eng, lambda e: e.tensor_copy(out, in_), R, W)

    def memset(self, eng, out, val, W):
        self.last_n = int(np.prod(out.shape[1:]))
        self._op(eng, lambda e: e.memset(out, val), [], W)

    def recip(self, out, in_, R, W):
        self.last_n = 6 * int(np.prod(out.shape[1:]))
        self._op("dve", lambda e: e.reciprocal(out, in_), R, W)

    def scan(self, out, d0, d1, R, W):
        self.last_n = int(np.prod(out.shape[1:]))
        self._op("dve", lambda e: e.tensor_tensor_scan(out, d0, d1, 0.0, ALU.mult, ALU.add), R, W)

    def mm(self, out, lhsT, rhs, start, stop, R, W, tp=None):
        self.last_n = int(np.prod(out.shape[1:]))
        if tp is None:
            self._op("pe", lambda e: e.matmul(out, lhsT, rhs, start=start, stop=stop), R, W)
        else:
            self._op("pe", lambda e: e.matmul(out, lhsT, rhs, start=start, stop=stop, tile_position=tp), R, W)

    def tr(self, out, in_, ident, R, W):
        self.last_n = int(np.prod(out.shape[1:]))
        self._op("pe", lambda e: e.transpose(out, in_, ident), R, W)

    def dma(self, q, out, in_, R, W):
        self._dma(q, lambda e: e.dma_start(out=out, in_=in_), R, W)


def bc(ap, shape):
    return ap.to_broadcast(list(shape))


def build_program(stop_after=None):
    nc = bass.Bass("TRN2", target_bir_lowering=False)
    k = K(nc)

    def din(name, shape, dt=F32):
        return nc.dram_tensor(name, list(shape), dt, kind="ExternalInput").ap()

    def dscr(name, shape, dt):
        return TT_(nc.dram_tensor(name, list(shape), dt, kind="Internal").ap(), name)

    xT = TT_(din("xT", [128, 8, S_LEN]), "xT")
    pT = TT_(din("pT", [4, 128, 2, S_LEN]), "pT")
    W = {}
    W["rkvg"] = din("a_w_rkvg", [2, 4, 128, 8, 1024])
    W["w1"] = din("a_w1", [2, 128, 8, 64])
    W["w2"] = din("a_w2", [2, 64, 1024])
    W["a1"] = din("a_a1", [2, 128, 8, 64])
    W["a2"] = din("a_a2", [2, 64, 1024])
    W["v1"] = din("a_v1", [1, 128, 8, 32])
    W["v2"] = din("a_v2", [1, 32, 1024])
    W["wo"] = din("a_w_o", [2, 128, 8, 1024])
    W["ple_w"] = din("ple_w", [4, 128, 2, 1024])
    W["ple_gate"] = din("ple_w_gate", [4, 128, 8, 1024])
    W["w_kv"] = din("w_kv", [128, 8, 2048])
    W["b_w_in"] = din("b_w_in", [2, 128, 8, 4096])
    W["b_w_o"] = din("b_w_o", [2, 128, 8, 1024])
    pvec_d = din("pvec", [128, NVEC * 8])
    cmask_d = din("cmask", [128, 4, 128])
    amask_d = din("amask", [128, 8, 256])
    outT = TT_(nc.dram_tensor("outT", [128, 8, S_LEN], F32, kind="ExternalOutput").ap(), "outT")

    hT = dscr("hT", [128, 8, S_LEN], F32)
    vfT = dscr("vfT", [NT, 128, 8 * T], F32)
    pkS = dscr("pkS", [NT, 128, 8 * 7 * T], BF16)
    edS = dscr("edS", [NT, 128, 16], F32)
    dram_w = TT_(None, "dram_w")

    pv = k.sb([128, NVEC * 8], F32, "pvec")
    k.dma("sp", pv[:], pvec_d, [], [pv])

    def vcol(name):
        i = VIDX[name]
        return pv[:, i * 8:(i + 1) * 8]

    cm32 = k.sb([128, 4, 128], F32, "cm32")
    k.dma("sp", cm32[:], cmask_d, [], [cm32])
    identb = k.sb([128, 128], BF16, "identb")
    k.cp("dve", identb[:], cm32[:, 0, :], [cm32], [identb])
    mask2 = k.sb([128, 256], BF16, "mask2")
    k.cp("dve", mask2[:, 0:128], cm32[:, 1, :], [cm32], [mask2])
    k.cp("dve", mask2[:, 128:256], cm32[:, 2, :], [cm32], [mask2])
    masklo = k.sb([128, 128], BF16, "masklo")
    k.cp("dve", masklo[:], cm32[:, 3, :], [cm32], [masklo])
    onesN = k.sb([128, 128], BF16, "onesN")
    k.memset("dve", onesN[:], 1.0 / 1024.0, [onesN])
    blk1 = k.sb([128, 128], BF16, "blk1")
    blk64 = k.sb([128, 128], BF16, "blk64")
    k.memset("dve", blk1[:], 0.0, [blk1])
    k.memset("dve", blk64[:], 0.0, [blk64])
    for hh in range(2):
        pr = slice(hh * 64, hh * 64 + 64)
        k.memset("dve", blk1[pr, pr], 1.0, [blk1])
        k.memset("dve", blk64[pr, pr], 1.0 / 64.0, [blk64])
    onesT = k.sb([128, T], F32, "onesT")
    k.memset("dve", onesT[:], 1.0, [onesT])
    epsN = k.sb([128, 1], F32, "epsN")
    k.memset("dve", epsN[:], 1e-6, [epsN])
    epsG = k.sb([128, 1], F32, "epsG")
    k.memset("dve", epsG[:], 64e-5, [epsG])
    stage = []
    stage_rr = [0]

    def with_stage(fn):
        m_ = k.mark()
        stage[:] = [k.sb([128, 1024], F32, "stage%d" % i) for i in range(3)]
        fn()
        k.release(m_)
        stage[:] = []

    def load_w(dst, dst_ap_fn, src_ap_fn, nk, ncols):
        for kc in range(nk):
            for c0 in range(0, ncols, 1024):
                cw = min(1024, ncols - c0)
                st = stage[stage_rr[0] % 3]
                stage_rr[0] += 1
                k.dma("sp", st[:, 0:cw], src_ap_fn(kc, c0, cw), [], [st])
                k.cp(k.alt(("act", "pool", "dve")), dst_ap_fn(kc, c0, cw), st[:, 0:cw], [st], [dst])

    def rmsnorm(src32, gname, out_ap, outT_, tmp32, sqb, rstd):
        k.act(sqb[:], src32[:], AF.Square, [src32], [sqb])
        p = k.ps()
        for c in range(8):
            k.mm(p[:, 0:T], onesN[:], sqb[:, c, :], c == 0, c == 7, [onesN, sqb], [p])
        k.act(rstd[:], p[:, 0:T], AF.Sqrt, [p, epsN], [rstd], bias=epsN[:, 0:1])
        k.recip(rstd[:], rstd[:], [rstd], [rstd])
        k.tt("dve", tmp32[:], src32[:], bc(rstd[:].unsqueeze(1), [128, 8, T]), ALU.mult, [src32, rstd], [tmp32])
        g_ = vcol(gname)
        for c in range(8):
            k.act(out_ap[:, c, :], tmp32[:, c, :], AF.Copy, [tmp32, pv], [outT_], scale=g_[:, c:c + 1])

    base_mark = k.mark()

    def rwkv_layer(li, src):
        m0 = k.mark()
        Wr = [k.sb([128, 8, 1024], BF16, "Wrkvg%d" % n) for n in range(4)]
        w1 = k.sb([128, 8, 64], BF16, "w1")
        a1 = k.sb([128, 8, 64], BF16, "a1")
        w2 = k.sb([64, 1024], BF16, "w2")
        a2 = k.sb([64, 1024], BF16, "a2")
        if li == 1:
            v1 = k.sb([128, 8, 32], BF16, "v1")
            v2 = k.sb([32, 1024], BF16, "v2")

        def load_A():
            for n in range(4):
                load_w(Wr[n], lambda kc, c0, cw, n=n: Wr[n][:, kc, c0:c0 + cw],
                       lambda kc, c0, cw, n=n: W["rkvg"][li, n, :, kc, c0:c0 + cw], 8, 1024)
            for (dst, key) in ((w1, "w1"), (a1, "a1")):
                st = stage[stage_rr[0] % 3]; stage_rr[0] += 1
                k.dma("sp", st[:, 0:512], W[key][li].rearrange("p c n -> p (c n)"), [], [st])
                k.cp("dve", dst[:].rearrange("p c n -> p (c n)"), st[:, 0:512], [st], [dst])
            for (dst, key) in ((w2, "w2"), (a2, "a2")):
                st = stage[stage_rr[0] % 3]; stage_rr[0] += 1
                k.dma("sp", st[0:64, :], W[key][li], [], [st])
                k.cp("dve", dst[:], st[0:64, :], [st], [dst])
            if li == 1:
                st = stage[stage_rr[0] % 3]; stage_rr[0] += 1
                k.dma("sp", st[:, 0:256], W["v1"][0].rearrange("p c n -> p (c n)"), [], [st])
                k.cp("dve", v1[:].rearrange("p c n -> p (c n)"), st[:, 0:256], [st], [v1])
                st = stage[stage_rr[0] % 3]; stage_rr[0] += 1
                k.dma("sp", st[0:32, :], W["v2"][0], [], [st])
                k.cp("dve", v2[:], st[0:32, :], [st], [v2])
        with_stage(load_A)

        L = str(li)
        h32 = k.sb([128, 8, T], F32, "h32")
        sqb = k.sb([128, 8, T], BF16, "sqb")
        rstd = k.sb([128, T], F32, "rstd")
        xnb = [k.sb([128, 8, T + 1], F32, "xnb%d" % i) for i in range(2)]
        xx = k.sb([128, 8, T], F32, "xx")
        tmpA = k.sb([128, 8, T], F32, "tmpA")
        xm = [k.sb([128, 8, T], BF16, "xm%d" % n) for n in range(6)]
        XB = []
        for sl_ in range(2):
            d_ = dict(r32=k.sb([128, 8, T], F32, "r32"), k32=k.sb([128, 8, T], F32, "k32"), v32=k.sb([128, 8, T], F32, "v32"),
                      sig=k.sb([128, 8, T], F32, "sig"), a32=k.sb([128, 8, T], F32, "a32"), gsb=k.sb([128, 8, T], BF16, "gsb"))
            if li == 1:
                d_["vm"] = k.sb([128, 8, T], F32, "vm")
            XB.append(d_)
        sqb2 = k.sb([128, 8, T], BF16, "sqb2")
        kk = k.sb([128, 8, T], F32, "kk")
        rn = k.sb([128, 8, T], F32, "rn")
        bb = k.sb([128, 8, T], F32, "bb")
        cs = k.sb([128, 8, T], F32, "cs")
        dd = k.sb([128, 8, T], F32, "dd")
        e_in = k.sb([128, 8, T], F32, "e_in")
        e_out = k.sb([128, 8, T], F32, "e_out")
        thb = k.sb([64, T], BF16, "thb")
        pk = k.sb([128, 8, 6, T], BF16, "pk")
        ed = k.sb([128, 16], F32, "ed")
        if li == 1:
            vf = dd
        k.memset("dve", xnb[0][:, :, 0:1], 0.0, [xnb[0]])

        def lora(xin, wA, wB, nA, func_mid, bias_name, out32):
            p = k.ps()
            for kc in range(8):
                k.mm(p[0:nA, 0:T], wA[:, kc, :], xin[:, kc, :], kc == 0, kc == 7, [wA, xin], [p])
            k.act(thb[0:nA, :], p[0:nA, 0:T], func_mid, [p], [thb])
            for hf in range(2):
                p2 = k.ps()
                for j in range(4):
                    oc = hf * 4 + j
                    k.mm(p2[:, j * T:(j + 1) * T], wB[:, oc * 128:(oc + 1) * 128], thb[0:nA, :], True, True, [wB, thb], [p2])
                cs_ = slice(hf * 4, hf * 4 + 4)
                k.tt("dve", tmpA[:, cs_, :], p2[:].rearrange("p (c t) -> p c t", c=4),
                     bc(vcol(bias_name)[:, cs_].unsqueeze(2), [128, 4, T]), ALU.add, [p2, pv], [tmpA])
                k.act(out32[:, cs_, :], tmpA[:, cs_, :], AF.Sigmoid, [tmpA], [out32])

        def frontA(t, sl):
            X_ = XB[sl]
            r32, k32, v32, sig, a32, gsb = (X_[x] for x in ("r32", "k32", "v32", "sig", "a32", "gsb"))
            t0 = t * T
            par = t % 2
            xn = xnb[par]
            k.dma("sp", h32[:], src[:, :, t0:t0 + T], [src], [h32])
            rmsnorm(h32, "a_ln_g" + L, xn[:, :, 1:T + 1], xn, tmpA, sqb, rstd)
            k.tt("pool", xx[:], xn[:, :, 0:T], xn[:, :, 1:T + 1], ALU.subtract, [xn], [xx])
            k.cp("pool", xnb[1 - par][:, :, 0:1], xn[:, :, T:T + 1], [xn], [xnb[1 - par]])
            for n in range(6):
                e1 = k.alt()
                tmp = tmpA if n % 2 == 0 else h32
                if e1 == "pool":
                    mu_ = vcol("a_mu%d_%d" % (li, n))
                    for c in range(8):
                        k.act(tmp[:, c, :], xx[:, c, :], AF.Copy, [xx, pv], [tmp], scale=mu_[:, c:c + 1])
                else:
                    k.tt(e1, tmp[:], xx[:], bc(vcol("a_mu%d_%d" % (li, n)).unsqueeze(2), [128, 8, T]), ALU.mult, [xx, pv], [tmp])
                k.tt(e1, xm[n][:], tmp[:], xn[:, :, 1:T + 1], ALU.add, [tmp, xn], [xm[n]])
            for n in range(4):
                for hf in range(2):
                    p = k.ps()
                    for j in range(4):
                        oc = hf * 4 + j
                        for kc in range(8):
                            k.mm(p[:, j * T:(j + 1) * T], Wr[n][:, kc, oc * 128:(oc + 1) * 128], xm[n][:, kc, :],
                                 kc == 0, kc == 7, [Wr[n], xm[n]], [p])
                    cs_ = slice(hf * 4, hf * 4 + 4)
                    pv3 = p[:].rearrange("p (c t) -> p c t", c=4)
                    if n == 0:
                        k.cp("act", r32[:, cs_, :], pv3, [p], [r32])
                    elif n == 1:
                        k.cp("dve", k32[:, cs_, :], pv3, [p], [k32])
                    elif n == 2:
                        k.cp("act", v32[:, cs_, :], pv3, [p], [v32])
                    else:
                        k.act(gsb[:, cs_, :], pv3, AF.Silu, [p], [gsb])
            lora(xm[4], w1, w2, 64, AF.Tanh, "a_w0" + L, sig)
            lora(xm[5], a1, a2, 64, AF.Copy, "a_a0" + L, a32)
            if li == 1:
                lora(xm[2], v1, v2, 32, AF.Copy, "a_v0", X_["vm"])

        def backA(t, sl):
            X_ = XB[sl]
            r32, k32, v32, sig, a32, gsb = (X_[x] for x in ("r32", "k32", "v32", "sig", "a32", "gsb"))
            if li == 0:
                k.dma("pool", vfT[t].rearrange("p (c t) -> p c t", c=8), v32[:], [v32], [vfT])
            else:
                vm = X_["vm"]
                k.dma("sp", vf[:], vfT[t].rearrange("p (c t) -> p c t", c=8), [vfT], [vf])
                k.tt("dve", vf[:], vf[:], v32[:], ALU.subtract, [vf, v32], [vf])
                k.tt("dve", vf[:], vf[:], vm[:], ALU.mult, [vf, vm], [vf])
                k.tt("dve", v32[:], v32[:], vf[:], ALU.add, [v32, vf], [v32])
            kkc_ = vcol("a_k_k" + L)
            for c in range(8):
                k.act(kk[:, c, :], k32[:, c, :], AF.Copy, [k32, pv], [kk], scale=kkc_[:, c:c + 1])
            k.act(sqb2[:], kk[:], AF.Square, [kk], [sqb2])
            for hf in range(2):
                p = k.ps()
                for j in range(4):
                    k.mm(p[:, j * T:(j + 1) * T], blk1[:], sqb2[:, hf * 4 + j, :], True, True, [blk1, sqb2], [p])
                k.act(rn[:, hf * 4:hf * 4 + 4, :], p[:].rearrange("p (c t) -> p c t", c=4), AF.Sqrt, [p], [rn])
            k.ts("dve", rn[:], rn[:], 1e-12, None, ALU.max, None, [rn], [rn])
            k.recip(rn[:], rn[:], [rn], [rn])
            k.tt("pool", kk[:], kk[:], rn[:], ALU.mult, [kk, rn], [kk])
            k.tt("pool", bb[:], kk[:], a32[:], ALU.mult, [kk, a32], [bb])
            k.stt("dve", a32[:], a32[:], -1.0, bc(vcol("a_k_a" + L).unsqueeze(2), [128, 8, T]), ALU.add, ALU.mult, [a32, pv], [a32])
            k.stt("dve", k32[:], a32[:], 1.0, k32[:], ALU.add, ALU.mult, [a32, k32], [k32])
            for c in range(8):
                k.scan(cs[:, c, :], onesT[:], sig[:, c, :], [onesT, sig], [cs])
            k.tt("pool", dd[:], cs[:], bc(cs[:, :, T // 2 - 1:T // 2], [128, 8, T]), ALU.subtract, [cs], [dd])
            k.act(e_in[:], dd[:], AF.Exp, [dd], [e_in], scale=-C0)
            k.act(e_out[:], dd[:], AF.Exp, [dd], [e_out], scale=C0)
            k.tt("pool", rn[:], dd[:], sig[:], ALU.subtract, [dd, sig], [rn])
            k.act(rn[:], rn[:], AF.Exp, [rn], [rn], scale=-C0)
            k.act(ed[:, 0:8], cs[:, :, T // 2 - 1], AF.Exp, [cs], [ed], scale=-C0)
            k.cp("dve", ed[:, 8:16], e_in[:, :, T - 1], [e_in], [ed])
            k.stt("dve", pk[:, :, 0, :], kk[:], -1.0, rn[:], ALU.mult, ALU.mult, [kk, rn], [pk])
            k.tt("pool", pk[:, :, 1, :], r32[:], e_in[:], ALU.mult, [r32, e_in], [pk])
            k.tt("dve", pk[:, :, 2, :], bb[:], e_out[:], ALU.mult, [bb, e_out], [pk])
            k.tt("pool", pk[:, :, 3, :], k32[:], e_out[:], ALU.mult, [k32, e_out], [pk])
            k.cp("act", pk[:, :, 4, :], v32[:], [v32], [pk])
            k.tt("dve", bb[:], r32[:], k32[:], ALU.mult, [r32, k32], [bb])
            rkc_ = vcol("a_r_k" + L)
            for c in range(8):
                k.act(pk[:, c, 5, :], bb[:, c, :], AF.Copy, [bb, pv], [pk], scale=rkc_[:, c:c + 1])
            pkS_t = pkS[t].rearrange("p (c q t) -> p c q t", c=8, q=7)
            k.dma("pool", pkS_t[:, :, 0:6, :], pk[:], [pk], [pkS])
            k.dma("pool", pkS_t[:, :, 6, :], gsb[:], [gsb], [pkS])
            k.dma("pool", edS[t], ed[:], [ed], [edS])

        frontA(0, 0)
        for t in range(NT):
            streams = [k.record(lambda: backA(t, t % 2), range(0, 4))]
            if t + 1 < NT:
                streams.append(k.record(lambda: frontA(t + 1, (t + 1) % 2), range(4, 8)))
            k.merge(streams)
        k.release(m0)
        if stop_after == "A":
            return

        Wo = k.sb([128, 8, 1024], BF16, "Wo")
        ple_load, ple_alloc, finish_layer = make_ple(li, 1)

        def load_B():
            load_w(Wo, lambda kc, c0, cw: Wo[:, kc, c0:c0 + cw], lambda kc, c0, cw: W["wo"][li, :, kc, c0:c0 + cw], 8, 1024)
            ple_load()
        with_stage(load_B)
        pqB = k.sb([128, 8, 3, T], BF16, "pqB")
        pk = k.sb([128, 8, 5, T], BF16, "pkB")
        h32 = k.sb([128, 8, T], F32, "h32B")
        LT = k.sb([128, 16, 128], BF16, "LT")
        Pb = [k.sb([128, 16, 128], BF16, "Pb%d" % i) for i in range(2)]
        PTb = [k.sb([128, 16, 128], BF16, "PTb%d" % i) for i in range(2)]
        Tb = [k.sb([128, 16, 128], BF16, "Tb%d" % i) for i in range(2)]
        XTs = k.sb([128, 1024], BF16, "XTs")
        UTs = k.sb([128, 1024], BF16, "UTs")
        S32 = k.sb([128, 8, 64], F32, "S32")
        S0m32 = k.sb([128, 8, 64], F32, "S0m32")
        y32 = k.sb([128, 8, T], F32, "y32")
        yb = k.sb([128, 8, T], BF16, "yb")
        ysq = k.sb([128, 8, T], BF16, "ysq")
        mean32 = k.sb([128, 8, T], F32, "mean32")
        var32 = k.sb([128, 8, T], F32, "var32")
        zb = k.sb([128, 8, T], BF16, "zb")
        h1 = y32
        ple_alloc(dict(tmp=var32, sg=mean32, ho=h32))
        k.memset("dve", S32[:], 0.0, [S32])
        UTz = k.sb([128, 8, 2, 2, 64], BF16, "UTz")
        S0bd = k.sb([128, 8, 2, 64], BF16, "S0bd")
        for z_ in (UTz, S0bd):
            k.memset("pool", z_[:], 0.0, [z_])
        PBs = []
        for sl in range(2):
            d_ = dict(ed=k.sb([128, 16], F32, "edB"), tok=[k.sb([128, 1024], BF16, "tok%d" % i) for i in range(3)],
                      LBs=k.sb([128, 16, 256], BF16, "LBs"), LKs=k.sb([128, 16, 256], BF16, "LKs"),
                      Tf=k.sb([128, 16, 128], BF16, "Tf"), ARz=k.sb([128, 8, 2, 2, T], BF16, "ARz"),
                      VTz=k.sb([128, 8, 2, 2, 64], BF16, "VTz"))
            for z_ in (d_["ARz"], d_["VTz"]):
                k.memset("pool", z_[:], 0.0, [z_])
            PBs.append(d_)

        def pre(t, sl):
            P_ = PBs[sl]
            ed, tok, LBs, LKs, Tf, ARz, VTz = (P_[x] for x in ("ed", "tok", "LBs", "LKs", "Tf", "ARz", "VTz"))
            BTt, KTt, VT = tok
            k.dma("sp", pk[:], pkS[t].rearrange("p (c q t) -> p c q t", c=8, q=7)[:, :, 0:5, :], [pkS], [pk])
            k.dma("sp", ed[:], edS[t], [edS], [ed])
            for qi, q in enumerate((2, 3, 4)):
                p = k.ps()
                pb = p[:].bitcast(BF16)
                for c in range(8):
                    k.tr(pb[:, c * 128:(c + 1) * 128], pk[:, c, q, :], identb[:], [pk, identb], [p])
                k.cp("act" if qi != 1 else "dve", tok[qi][:], pb, [p], [tok[qi]])
            for hh in range(2):
                pr = slice(hh * 64, hh * 64 + 64)
                k.cp("pool", ARz[pr, :, hh, :, :], pk[pr, :, 0:2, :], [pk], [ARz])
                k.cp("pool", VTz[:, :, hh, hh, :], VT[:].rearrange("p (c h i) -> p c h i", c=8, h=2)[:, :, hh, :], [VT], [VTz])
            for c in range(8):
                pLB = k.ps(); pLK = k.ps(); pLT = k.ps()
                for hh in range(2):
                    rhsAR = ARz[:, c, hh, :, :].rearrange("p a t -> p (a t)")
                    k.mm(pLB[:, hh * 256:(hh + 1) * 256], pk[:, c, 2, :], rhsAR, True, True, [pk, ARz], [pLB])
                    k.mm(pLK[:, hh * 256:(hh + 1) * 256], pk[:, c, 3, :], rhsAR, True, True, [pk, ARz], [pLK])
                    k.mm(pLT[:, hh * 128:(hh + 1) * 128], ARz[:, c, hh, 0, :], pk[:, c, 2, :], True, True, [pk, ARz], [pLT])
                k.tt("dve", LBs[:, 2 * c:2 * c + 2, :], pLB[:].rearrange("p (h x) -> p h x", h=2),
                     bc(mask2[:].unsqueeze(1), [128, 2, 256]), ALU.mult, [pLB, mask2], [LBs])
                k.tt("dve", LKs[:, 2 * c:2 * c + 2, :], pLK[:].rearrange("p (h x) -> p h x", h=2),
                     bc(mask2[:].unsqueeze(1), [128, 2, 256]), ALU.mult, [pLK, mask2], [LKs])
                k.tt("dve", LT[:, 2 * c:2 * c + 2, :], pLT[:, 0:256].rearrange("p (h x) -> p h x", h=2),
                     bc(masklo[:].unsqueeze(1), [128, 2, 128]), ALU.mult, [pLT, masklo], [LT])
            k.tt("pool", Tb[0][:], LBs[:, :, 0:128], bc(identb[:].unsqueeze(1), [128, 16, 128]), ALU.add, [LBs, identb], [Tb[0]])
            Pc_ap = lambda h: LBs[:, h, 0:128]
            PTc_ap = lambda h: LT[:, h, :]
            Pc_t, PTc_t = LBs, LT
            Tc = 0
            for lvl in range(1, 7):
                Pn, PTn = Pb[lvl % 2], PTb[lvl % 2]
                for g in range(4):
                    p1 = k.ps(); p2 = k.ps()
                    for j in range(4):
                        h = g * 4 + j
                        k.mm(p1[:, j * 128:(j + 1) * 128], PTc_ap(h), Pc_ap(h), True, True, [Pc_t, PTc_t], [p1])
                    for j in range(4):
                        h = g * 4 + j
                        k.mm(p2[:, j * 128:(j + 1) * 128], Pc_ap(h), PTc_ap(h), True, True, [Pc_t, PTc_t], [p2])
                    k.cp("act", Pn[:, g * 4:g * 4 + 4, :], p1[:].rearrange("p (h x) -> p h x", h=4), [p1], [Pn])
                    k.cp("act", PTn[:, g * 4:g * 4 + 4, :], p2[:].rearrange("p (h x) -> p h x", h=4), [p2], [PTn])
                Told = Tb[Tc]
                Tnew = Tb[1 - Tc] if lvl < 6 else Tf
                for g in range(4):
                    p3 = k.ps()
                    for j in range(4):
                        h = g * 4 + j
                        k.mm(p3[:, j * 128:(j + 1) * 128], PTn[:, h, :], Told[:, h, :], True, True, [PTn, Told], [p3])
                    k.tt("dve", Tnew[:, g * 4:g * 4 + 4, :], p3[:].rearrange("p (h x) -> p h x", h=4),
                         Told[:, g * 4:g * 4 + 4, :], ALU.add, [p3, Told], [Tnew])
                Tc = 1 - Tc
                Pc_t, PTc_t = Pn, PTn
                Pc_ap = lambda h, Pn=Pn: Pn[:, h, :]
                PTc_ap = lambda h, PTn=PTn: PTn[:, h, :]

        def chain(t, sl):
            P_ = PBs[sl]
            ed, tok, LBs, LKs, Tfin, ARz, VTz = (P_[x] for x in ("ed", "tok", "LBs", "LKs", "Tf", "ARz", "VTz"))
            BTt, KTt, VT = tok
            k.tt("dve", S0m32[:], S32[:], bc(ed[:, 0:8].unsqueeze(2), [128, 8, 64]), ALU.mult, [S32, ed], [S0m32])
            for hh in range(2):
                pr = slice(hh * 64, hh * 64 + 64)
                k.cp("act", S0bd[pr, :, hh, :], S0m32[pr, :, :], [S0m32], [S0bd])
            for u in range(2):
                p = k.ps()
                for j in range(8):
                    h = u * 8 + j
                    c, hh = h // 2, h % 2
                    k.mm(p[:, j * 64:(j + 1) * 64], ARz[:, c, hh, 0, :], S0bd[:, c, hh, :], True, False, [ARz, S0bd], [p])
                    k.mm(p[:, j * 64:(j + 1) * 64], LKs[:, h, 0:128], VT[:, h * 64:(h + 1) * 64], False, True, [LKs, VT], [p])
                k.cp("act", XTs[:, u * 512:(u + 1) * 512], p[:], [p], [XTs])
            for u in range(2):
                p = k.ps()
                for j in range(8):
                    h = u * 8 + j
                    k.mm(p[:, j * 64:(j + 1) * 64], Tfin[:, h, :], XTs[:, h * 64:(h + 1) * 64], True, True, [Tfin, XTs], [p])
                k.cp("act", UTs[:, u * 512:(u + 1) * 512], p[:], [p], [UTs])
                p4 = p[:].rearrange("p (c h i) -> p c h i", c=4, h=2)
                for hh in range(2):
                    k.cp("dve", UTz[:, 4 * u:4 * u + 4, hh, hh, :], p4[:, :, hh, :], [p], [UTz])
            pS = []
            for u in range(2):
                p = k.ps()
                for j in range(8):
                    h = u * 8 + j
                    c = h // 2
                    k.mm(p[:, j * 64:(j + 1) * 64], BTt[:, c * 128:(c + 1) * 128], UTs[:, h * 64:(h + 1) * 64], True, False, [BTt, UTs], [p])
                    k.mm(p[:, j * 64:(j + 1) * 64], KTt[:, c * 128:(c + 1) * 128], VT[:, h * 64:(h + 1) * 64], False, True, [KTt, VT], [p])
                pS.append(p)
            for u in range(2):
                p4 = pS[u][:].rearrange("p (c h i) -> p c h i", c=4, h=2)
                for hh in range(2):
                    pr = slice(hh * 64, hh * 64 + 64)
                    k.tt("dve", S0m32[pr, 4 * u:4 * u + 4, :], S0m32[pr, 4 * u:4 * u + 4, :], p4[pr, :, hh, :], ALU.add, [S0m32, pS[u]], [S0m32])
            k.tt("dve", S32[:], S0m32[:], bc(ed[:, 8:16].unsqueeze(2), [128, 8, 64]), ALU.mult, [S0m32, ed], [S32])
            for hf in range(2):
                p = k.ps()
                for j in range(4):
                    c = hf * 4 + j
                    o = p[:, j * T:(j + 1) * T]
                    for hh in range(2):
                        k.mm(o, S0bd[:, c, :, :].rearrange("p h i -> p (h i)"), ARz[:, c, hh, 1, :], hh == 0, False, [S0bd, ARz], [p])
                    for hh in range(2):
                        h = 2 * c + hh
                        k.mm(o, UTz[:, c, hh, :, :].rearrange("p h i -> p (h i)"), LBs[:, h, 128:256], False, False, [UTz, LBs], [p])
                        k.mm(o, VTz[:, c, hh, :, :].rearrange("p h i -> p (h i)"), LKs[:, h, 128:256], False, hh == 1, [VTz, LKs], [p])
                cs_ = slice(hf * 4, hf * 4 + 4)
                pv3 = p[:].rearrange("p (c t) -> p c t", c=4)
                k.cp("act", y32[:, cs_, :], pv3, [p], [y32])
                k.act(ysq[:, cs_, :], pv3, AF.Square, [p], [ysq])
                k.cp("dve", yb[:, cs_, :], pv3, [p], [yb])

        def post(t):
            t0 = t * T
            k.dma("sp", pqB[:], pkS[t].rearrange("p (c q t) -> p c q t", c=8, q=7)[:, :, 4:7, :], [pkS], [pqB])
            k.dma("sp", h32[:], src[:, :, t0:t0 + T], [src], [h32])
            for hf in range(2):
                cs_ = slice(hf * 4, hf * 4 + 4)
                pm = k.ps(); pq = k.ps()
                for j in range(4):
                    k.mm(pm[:, j * T:(j + 1) * T], blk64[:], yb[:, hf * 4 + j, :], True, True, [blk64, yb], [pm])
                for j in range(4):
                    k.mm(pq[:, j * T:(j + 1) * T], blk64[:], ysq[:, hf * 4 + j, :], True, True, [blk64, ysq], [pq])
                k.cp("act", mean32[:, cs_, :], pm[:].rearrange("p (c t) -> p c t", c=4), [pm], [mean32])
                k.tt("pool", var32[:, cs_, :], mean32[:, cs_, :], mean32[:, cs_, :], ALU.mult, [mean32], [var32])
                k.tt("dve", var32[:, cs_, :], pq[:].rearrange("p (c t) -> p c t", c=4), var32[:, cs_, :], ALU.subtract, [pq, var32], [var32])
            k.ts("dve", var32[:], var32[:], 0.0, None, ALU.max, None, [var32], [var32])
            k.act(var32[:], var32[:], AF.Sqrt, [var32, epsG], [var32], bias=epsG[:, 0:1])
            k.recip(var32[:], var32[:], [var32], [var32])
            k.tt("pool", y32[:], y32[:], mean32[:], ALU.subtract, [y32, mean32], [y32])
            k.tt("dve", y32[:], y32[:], var32[:], ALU.mult, [y32, var32], [y32])
            k.tt("pool", y32[:], y32[:], bc(vcol("a_gn_g" + L).unsqueeze(2), [128, 8, T]), ALU.mult, [y32, pv], [y32])
            k.tt("pool", y32[:], y32[:], bc(vcol("a_gn_b" + L).unsqueeze(2), [128, 8, T]), ALU.add, [y32, pv], [y32])
            for hf in range(2):
                cs_ = slice(hf * 4, hf * 4 + 4)
                pr_ = k.ps()
                for j in range(4):
                    k.mm(pr_[:, j * T:(j + 1) * T], blk1[:], pqB[:, hf * 4 + j, 1, :], True, True, [blk1, pqB], [pr_])
                k.tt("dve", mean32[:, cs_, :], pr_[:].rearrange("p (c t) -> p c t", c=4), pqB[:, cs_, 0, :], ALU.mult, [pr_, pqB], [mean32])
            k.tt("pool", y32[:], y32[:], mean32[:], ALU.add, [y32, mean32], [y32])
            k.tt("dve", zb[:], y32[:], pqB[:, :, 2, :], ALU.mult, [y32, pqB], [zb])
            for hf in range(2):
                cs_ = slice(hf * 4, hf * 4 + 4)
                p = k.ps()
                for j in range(4):
                    oc = hf * 4 + j
                    for kc in range(8):
                        k.mm(p[:, j * T:(j + 1) * T], Wo[:, kc, oc * 128:(oc + 1) * 128], zb[:, kc, :], kc == 0, kc == 7, [Wo, zb], [p])
                k.tt("dve", h1[:, cs_, :], p[:].rearrange("p (c t) -> p c t", c=4), h32[:, cs_, :], ALU.add, [p, h32], [h1])
            finish_layer(h1, t)

        def seq(t, sl):
            chain(t, sl)
            post(t)

        pre(0, 0)
        for t in range(NT):
            streams = [k.record(lambda: seq(t, t % 2), range(0, 4))]
            if t + 1 < NT:
                streams.append(k.record(lambda: pre(t + 1, (t + 1) % 2), range(4, 8)))
            k.merge(streams)
        k.release(m0)

    def make_ple(li, nslots=1):
        Wg = k.sb([128, 8, 1024], BF16, "Wg")
        Wp = k.sb([128, 2, 1024], BF16, "Wp")

        def load():
            load_w(Wg, lambda kc, c0, cw: Wg[:, kc, c0:c0 + cw], lambda kc, c0, cw: W["ple_gate"][li, :, kc, c0:c0 + cw], 8, 1024)
            load_w(Wp, lambda kc, c0, cw: Wp[:, kc, c0:c0 + cw], lambda kc, c0, cw: W["ple_w"][li, :, kc, c0:c0 + cw], 2, 1024)

        B = []

        def alloc(shared=None):
            for sl in range(nslots):
                d_ = dict(sqb=k.sb([128, 8, T], BF16, "sqbP"), rstd=k.sb([128, T], F32, "rstdP"),
                          n2=k.sb([128, 8, T], BF16, "n2"), p32=k.sb([128, 2, T], F32, "p32"), pb=k.sb([128, 2, T], BF16, "pbP"))
                for nm in ("tmp", "sg", "ho"):
                    d_[nm] = shared[nm] if shared is not None else k.sb([128, 8, T], F32, nm + "P")
                B.append(d_)

        def fn(h1, t, slot=0):
            b_ = B[slot]
            sqb, rstd, tmp, n2, sg, p32, pb, ho = (b_[x] for x in ("sqb", "rstd", "tmp", "n2", "sg", "p32", "pb", "ho"))
            t0 = t * T
            k.dma("sp", p32[:], pT[li, :, :, t0:t0 + T], [pT], [p32])
            k.cp("pool", pb[:], p32[:], [p32], [pb])
            rmsnorm(h1, "ple_g%d" % li, n2[:], n2, tmp, sqb, rstd)
            for hf in range(2):
                cs_ = slice(hf * 4, hf * 4 + 4)
                p = k.ps()
                for j in range(4):
                    oc = hf * 4 + j
                    for kc in range(8):
                        k.mm(p[:, j * T:(j + 1) * T], Wg[:, kc, oc * 128:(oc + 1) * 128], n2[:, kc, :], kc == 0, kc == 7, [Wg, n2], [p])
                k.act(sg[:, cs_, :], p[:].rearrange("p (c t) -> p c t", c=4), AF.Sigmoid, [p], [sg])
                p2 = k.ps()
                for j in range(4):
                    oc = hf * 4 + j
                    for kc in range(2):
                        k.mm(p2[:, j * T:(j + 1) * T], Wp[:, kc, oc * 128:(oc + 1) * 128], pb[:, kc, :], kc == 0, kc == 1, [Wp, pb], [p2])
                k.tt("dve", sg[:, cs_, :], p2[:].rearrange("p (c t) -> p c t", c=4), sg[:, cs_, :], ALU.mult, [p2, sg], [sg])
                k.tt("pool", ho[:, cs_, :], sg[:, cs_, :], h1[:, cs_, :], ALU.add, [sg, h1], [ho])
            k.dma("pool", hT[:, :, t0:t0 + T], ho[:], [ho], [hT])
        return load, alloc, fn

    def run_pairs(tile_fn, n):
        for t0 in range(0, n, 2):
            streams = [k.record(lambda t=t: tile_fn(t, t - t0), range(4 * (t - t0), 4 * (t - t0) + 4)) for t in range(t0, min(n, t0 + 2))]
            k.merge(streams)

    KTs = dscr("KTs", [128, 8, S_LEN], BF16)
    Vtok = dscr("Vtok", [S_LEN, 1024], BF16)
    OGs = dscr("OGs", [128, 8, S_LEN], BF16)

    def kv_phase():
        m0 = k.mark()
        Wkv = k.sb([128, 8, 2048], BF16, "Wkv")
        with_stage(lambda: load_w(Wkv, lambda kc, c0, cw: Wkv[:, kc, c0:c0 + cw], lambda kc, c0, cw: W["w_kv"][:, kc, c0:c0 + cw], 8, 2048))
        B = [dict(h32=k.sb([128, 8, T], F32, "h32K"), sqb=k.sb([128, 8, T], BF16, "sqbK"), rstd=k.sb([128, T], F32, "rstdK"),
                  tmp=k.sb([128, 8, T], F32, "tmpK"), nb=k.sb([128, 8, T], BF16, "nbK"), kT=k.sb([128, 8, T], BF16, "kTK"),
                  vt=k.sb([128, 1024], BF16, "vtK")) for _ in range(2)]

        def tile(t, slot):
            b_ = B[slot]
            h32, sqb, rstd, tmp, nb_, kT, vt = (b_[x] for x in ("h32", "sqb", "rstd", "tmp", "nb", "kT", "vt"))
            t0 = t * T
            k.dma("sp", h32[:], hT[:, :, t0:t0 + T], [hT], [h32])
            rmsnorm(h32, "kv_ln_g", nb_[:], nb_, tmp, sqb, rstd)
            for hf in range(2):
                p = k.ps()
                for j in range(4):
                    oc = hf * 4 + j
                    for kc in range(8):
                        k.mm(p[:, j * T:(j + 1) * T], Wkv[:, kc, oc * 128:(oc + 1) * 128], nb_[:, kc, :], kc == 0, kc == 7, [Wkv, nb_], [p])
                k.cp("act", kT[:, hf * 4:hf * 4 + 4, :], p[:].rearrange("p (c t) -> p c t", c=4), [p], [kT])
            k.dma("pool", KTs[:, :, t0:t0 + T], kT[:], [kT], [KTs])
            for hf in range(2):
                p = k.ps()
                for kc in range(8):
                    k.mm(p[:, 0:512], nb_[:, kc, :], Wkv[:, kc, 1024 + hf * 512:1024 + (hf + 1) * 512], kc == 0, kc == 7, [Wkv, nb_], [p])
                k.cp("dve", vt[:, hf * 512:(hf + 1) * 512], p[:], [p], [vt])
            k.dma("pool", Vtok[t0:t0 + T, :], vt[:], [vt], [Vtok])
        run_pairs(tile, NT)
        k.release(m0)

    DIL = (1, 4, 16)

    def attn_layer(li):
        j_ = li - 2
        m0 = k.mark()
        xnT = k.sb([128, 8, S_LEN], BF16, "xnT")
        m1 = k.mark()
        B1 = [dict(h32=k.sb([128, 8, T], F32, "h32A"), sqb=k.sb([128, 8, T], BF16, "sqbA"), rstd=k.sb([128, T], F32, "rstdA"),
                   tmp=k.sb([128, 8, T], F32, "tmpA2")) for _ in range(2)]

        def tile1(t, slot):
            b_ = B1[slot]
            t0 = t * T
            k.dma("sp", b_["h32"][:], hT[:, :, t0:t0 + T], [hT], [b_["h32"]])
            rmsnorm(b_["h32"], "b_ln_g%d" % j_, xnT[:, :, t0:t0 + T], xnT, b_["tmp"], b_["sqb"], b_["rstd"])
        run_pairs(tile1, NT)
        k.release(m1)
        stage[:] = [k.sb([128, 1024], F32, "stageA%d" % i) for i in range(3)]
        am32 = k.sb([128, 8, 256], F32, "am32")
        k.dma("sp", am32[:], amask_d, [], [am32])
        amb = k.sb([128, 8, 256], BF16, "amb")
        k.cp("dve", amb[:], am32[:], [am32], [amb])
        onesb = k.sb([128, 128], BF16, "onesb")
        k.memset("dve", onesb[:], 1.0, [onesb])
        Wh = k.sb([128, 8, 4, 128], BF16, "Wh")
        Kp = [k.sb([128, S_LEN], BF16, "Kp%d" % g) for g in range(3)]
        Qp = [k.sb([128, S_LEN], BF16, "Qp%d" % g) for g in range(3)]
        sgh = k.sb([128, S_LEN], BF16, "sgh")
        Vp = k.sb([128, 32, 128], BF16, "Vp")
        Oacc = k.sb([128, 2, S_LEN], F32, "Oacc")
        Pb = [k.sb([128, 256], BF16, "Pb%d" % i) for i in range(4)]
        ogh = Kp[2]
        qscale = float(128 ** -0.5)
        for h in range(8):
            for g in range(4):
                st = stage[stage_rr[0] % 3]; stage_rr[0] += 1
                col = (g * 1024 if g < 3 else 3072) + h * 128
                k.dma("sp", st[:].rearrange("p (c n) -> p c n", c=8), W["b_w_in"][j_, :, :, col:col + 128], [], [st])
                k.cp(k.alt(("act", "pool")), Wh[:, :, g, :], st[:].rearrange("p (c n) -> p c n", c=8), [st], [Wh])
            k.dma("sp", Kp[0][:], KTs[:, h, :], [KTs], [Kp[0]])
            for g in (1, 2):
                d = DIL[g]
                k.cp("pool", Kp[g][:].rearrange("p (r n) -> p r n", r=d), Kp[0][:].rearrange("p (n r) -> p r n", r=d), [Kp[0]], [Kp[g]])
            for tt in range(8):
                ts_ = slice(tt * 512, (tt + 1) * 512)
                for g in range(3):
                    d = DIL[g]
                    p = k.ps()
                    for kc in range(8):
                        k.mm(p[:, 0:512], Wh[:, kc, g, :], xnT[:, kc, ts_], kc == 0, kc == 7, [Wh, xnT], [p])
                    k.act(Qp[g][:].rearrange("p (r n) -> p r n", r=d)[:, :, tt * 512 // d:(tt + 1) * 512 // d],
                          p[:, 0:512].rearrange("p (n r) -> p r n", r=d), AF.Copy, [p], [Qp[g]], scale=qscale)
                p = k.ps()
                for kc in range(8):
                    k.mm(p[:, 0:512], Wh[:, kc, 3, :], xnT[:, kc, ts_], kc == 0, kc == 7, [Wh, xnT], [p])
                k.act(sgh[:, ts_], p[:, 0:512], AF.Silu, [p], [sgh])
            blocks = []
            for g in range(3):
                d = DIL[g]
                nb = 32 // d
                for r in range(d):
                    for c in range(nb):
                        blocks.append((g, d, nb, r, c))
            state = {"g": -1}

            def qk_stage(i):
                g, d, nb, r, c = blocks[i]
                blk = r * nb + c
                cols = slice(blk * 128, (blk + 1) * 128)
                pcols = slice((blk - 1) * 128, blk * 128)
                P_ = Pb[i % 4]
                ps_s = k.ps()
                k.mm(ps_s[:, 128:256], Kp[g][:, cols], Qp[g][:, cols], True, False, [Kp[g], Qp[g]], [ps_s])
                k.mm(ps_s[:, 128:256], identb[:], amb[:, h, 128:256], False, True, [identb, amb], [ps_s])
                if c > 0:
                    k.mm(ps_s[:, 0:128], Kp[g][:, pcols], Qp[g][:, cols], True, False, [Kp[g], Qp[g]], [ps_s])
                    k.mm(ps_s[:, 0:128], identb[:], amb[:, h, 0:128], False, True, [identb, amb], [ps_s])
                lo = 0 if c > 0 else 128
                k.act(P_[:, lo:256], ps_s[:, lo:256], AF.Exp, [ps_s], [P_])

            def pv_stage(i):
                g, d, nb, r, c = blocks[i]
                blk = r * nb + c
                P_ = Pb[i % 4]
                if state["g"] != g:
                    for r2 in range(d):
                        src = Vtok.h.rearrange("(c j r) f -> r j c f", j=128, r=d)[r2, :, :, h * 128:(h + 1) * 128]
                        k.dma("sp", Vp[:, r2 * nb:(r2 + 1) * nb, :], src, [Vtok], [Vp])
                    state["g"] = g
                Ov = Oacc[:].rearrange("p a (n r) -> p a r n", r=d)
                ps_o = k.ps()
                if c > 0:
                    k.mm(ps_o[:, 0:128], Vp[:, blk - 1, :], P_[:, 0:128], True, False, [Vp, P_], [ps_o])
                k.mm(ps_o[:, 0:128], Vp[:, blk, :], P_[:, 128:256], c == 0, True, [Vp, P_], [ps_o])
                if c > 0:
                    k.mm(ps_o[:, 128:256], onesb[:], P_[:, 0:128], True, False, [onesb, P_], [ps_o])
                k.mm(ps_o[:, 128:256], onesb[:], P_[:, 128:256], c == 0, True, [onesb, P_], [ps_o])
                dst = Ov[:, :, r, c * 128:(c + 1) * 128]
                srcp = ps_o[:, 0:256].rearrange("p (a i) -> p a i", a=2)
                if g == 0:
                    k.cp("dve", dst, srcp, [ps_o], [Oacc])
                else:
                    k.tt("dve", dst, srcp, dst, ALU.add, [ps_o, Oacc], [Oacc])

            LOOK = 2
            for i in range(len(blocks) + LOOK):
                if i < len(blocks):
                    qk_stage(i)
                if i - LOOK >= 0:
                    pv_stage(i - LOOK)
            k.recip(Oacc[:, 1, :], Oacc[:, 1, :], [Oacc], [Oacc])
            k.tt("dve", Oacc[:, 0, :], Oacc[:, 0, :], Oacc[:, 1, :], ALU.mult, [Oacc], [Oacc])
            k.tt("pool", ogh[:], Oacc[:, 0, :], sgh[:], ALU.mult, [Oacc, sgh], [ogh])
            k.dma("pool", OGs[:, h, :], ogh[:], [ogh], [OGs])
        k.release(m0)
        stage[:] = []
        Wo = k.sb([128, 8, 1024], BF16, "WoA")
        ple_load, ple_alloc, finish_layer = make_ple(li, 2)

        def load_C():
            load_w(Wo, lambda kc, c0, cw: Wo[:, kc, c0:c0 + cw], lambda kc, c0, cw: W["b_w_o"][j_, :, kc, c0:c0 + cw], 8, 1024)
            ple_load()
        with_stage(load_C)
        ple_alloc()
        B3 = [dict(h32=k.sb([128, 8, T], F32, "h32C"), og=k.sb([128, 8, T], BF16, "ogC"), h1=k.sb([128, 8, T], F32, "h1C")) for _ in range(2)]

        def tile3(t, slot):
            b_ = B3[slot]
            h32, og, h1 = b_["h32"], b_["og"], b_["h1"]
            t0 = t * T
            k.dma("sp", h32[:], hT[:, :, t0:t0 + T], [hT], [h32])
            k.dma("sp", og[:], OGs[:, :, t0:t0 + T], [OGs], [og])
            for hf in range(2):
                cs_ = slice(hf * 4, hf * 4 + 4)
                p = k.ps()
                for j in range(4):
                    oc = hf * 4 + j
                    for kc in range(8):
                        k.mm(p[:, j * T:(j + 1) * T], Wo[:, kc, oc * 128:(oc + 1) * 128], og[:, kc, :], kc == 0, kc == 7, [Wo, og], [p])
                k.tt("dve", h1[:, cs_, :], p[:].rearrange("p (c t) -> p c t", c=4), h32[:, cs_, :], ALU.add, [p, h32], [h1])
            finish_layer(h1, t, slot)
        run_pairs(tile3, NT)
        k.release(m0)

    def final_phase(gname="final_ln_g"):
        m0 = k.mark()
        B = [dict(h32=k.sb([128, 8, T], F32, "h32F"), sqb=k.sb([128, 8, T], BF16, "sqbF"), rstd=k.sb([128, T], F32, "rstdF"),
                  tmp=k.sb([128, 8, T], F32, "tmpF"), o32=k.sb([128, 8, T], F32, "o32F")) for _ in range(2)]

        def tile(t, slot):
            b_ = B[slot]
            t0 = t * T
            k.dma("sp", b_["h32"][:], hT[:, :, t0:t0 + T], [hT], [b_["h32"]])
            rmsnorm(b_["h32"], gname, b_["o32"][:], b_["o32"], b_["tmp"], b_["sqb"], b_["rstd"])
            k.dma("pool", outT[:, :, t0:t0 + T], b_["o32"][:], [b_["o32"]], [outT])
        run_pairs(tile, NT)
        k.release(m0)

    def dump_h():
        m0 = k.mark()
        h32 = k.sb([128, 8, T], F32, "h32D")
        for t in range(NT):
            k.dma("sp", h32[:], hT[:, :, t * T:(t + 1) * T], [hT], [h32])
            k.dma("pool", outT[:, :, t * T:(t + 1) * T], h32[:], [h32], [outT])
        k.release(m0)

    rwkv_layer(0, xT)
    if isinstance(stop_after, str):
        pass
    elif stop_after == 0:
        dump_h()
    else:
        rwkv_layer(1, hT)
        if stop_after == 1:
            dump_h()
        else:
            kv_phase()
            attn_layer(2)
            if stop_after == 2:
                dump_h()
            else:
                attn_layer(3)
                if stop_after == 3:
                    dump_h()
                else:
                    final_phase()
    k.s.barrier()
    k.s.emit()
    return nc


def _fm(w):
    K_, N_ = w.shape
    return np.ascontiguousarray(w.reshape(K_ // 128, 128, N_).transpose(1, 0, 2))


def _vec(v):
    return np.asarray(v, np.float32).reshape(8, 128).T


def host_inputs(inp):
    f = lambda a: np.asarray(a, np.float32)
    common = {}
    common["a_w_rkvg"] = np.stack([np.stack([_fm(f(inp["a_w_rkvg"])[i, n]) for n in range(4)]) for i in range(2)])
    common["a_w1"] = np.stack([_fm(f(inp["a_w1"])[i]) for i in range(2)])
    common["a_w2"] = np.ascontiguousarray(f(inp["a_w2"]))
    common["a_a1"] = np.stack([_fm(f(inp["a_a1"])[i]) for i in range(2)])
    common["a_a2"] = np.ascontiguousarray(f(inp["a_a2"]))
    common["a_v1"] = np.stack([_fm(f(inp["a_v1"])[0])])
    common["a_v2"] = np.ascontiguousarray(f(inp["a_v2"]))
    common["a_w_o"] = np.stack([_fm(f(inp["a_w_o"])[i]) for i in range(2)])
    common["ple_w"] = np.stack([_fm(f(inp["ple_w"])[i]) for i in range(4)])
    common["ple_w_gate"] = np.stack([_fm(f(inp["ple_w_gate"])[i]) for i in range(4)])
    common["w_kv"] = _fm(f(inp["w_kv"]))
    common["b_w_in"] = np.stack([_fm(f(inp["b_w_in"])[i]) for i in range(2)])
    common["b_w_o"] = np.stack([_fm(f(inp["b_w_o"])[i]) for i in range(2)])
    vecs = {}
    for i in range(2):
        vecs["a_ln_g%d" % i] = inp["a_ln_g"][i]
        for n in range(6):
            vecs["a_mu%d_%d" % (i, n)] = inp["a_mu"][i, n]
        for s in ("a_w0", "a_a0", "a_k_k", "a_k_a", "a_gn_g", "a_gn_b", "b_ln_g"):
            vecs[s + str(i)] = inp[s][i]
        vecs["a_r_k%d" % i] = np.asarray(inp["a_r_k"][i]).reshape(-1)
    vecs["a_v0"] = inp["a_v0"][0]
    vecs["kv_ln_g"] = inp["kv_ln_g"]
    vecs["final_ln_g"] = inp["final_ln_g"]
    for i in range(4):
        vecs["ple_g%d" % i] = inp["ple_gate_ln_g"][i]
    common["pvec"] = np.ascontiguousarray(np.concatenate([_vec(vecs[n]) for n in VEC_NAMES], axis=1))
    ii = np.arange(128)
    cm = np.zeros((128, 4, 128), np.float32)
    cm[:, 0, :] = np.eye(128)
    cm[:, 1, :] = (ii[:, None] < ii[None, :])
    cm[:, 2, :] = (ii[:, None] <= ii[None, :])
    cm[:, 3, :] = (ii[:, None] > ii[None, :])
    common["cmask"] = cm
    am = np.zeros((128, 8, 256), np.float32)
    for h in range(8):
        slope = 2.0 ** (-(h + 1))
        dprev = 128 + ii[None, :] - ii[:, None]
        dcur = ii[None, :] - ii[:, None]
        am[:, h, 0:128] = np.where((dprev >= 0) & (dprev <= 128), -slope * dprev, -30000.0)
        am[:, h, 128:256] = np.where((dcur >= 0) & (dcur <= 128), -slope * dcur, -30000.0)
    common["amask"] = am
    x = f(inp["x"])
    p = f(inp["p"])
    in_maps = []
    for b in range(8):
        m = dict(common)
        m["xT"] = np.ascontiguousarray(x[b].T.reshape(8, 128, S_LEN).transpose(1, 0, 2))
        m["pT"] = np.ascontiguousarray(p[:, b].transpose(0, 2, 1).reshape(4, 2, 128, S_LEN).transpose(0, 2, 1, 3))
        in_maps.append(m)
    return in_maps


_NC_CACHE = {}


def run(inp, stop_after=None, cores=8):
    in_maps = host_inputs(inp)[:cores]
    if stop_after not in _NC_CACHE:
        _NC_CACHE[stop_after] = build_program(stop_after)
    nc = _NC_CACHE[stop_after]
    res = run_bass_kernel_spmd(nc, in_maps, core_ids=list(range(cores)))
    outs = []
    for r in res.results:
        o = r["outT"]
        outs.append(o.transpose(1, 0, 2).reshape(D, S_LEN).T)
    return np.ascontiguousarray(np.stack(outs)).astype(np.float32)


def kernel(**inputs):
    return run(inputs)
```

```python
import numpy as np
import concourse.bass as bass
import concourse.mybir as mybir
from concourse.bass_utils import run_bass_kernel_spmd

F32 = mybir.dt.float32
BF16 = mybir.dt.bfloat16
ALU = mybir.AluOpType
AF = mybir.ActivationFunctionType

S_LEN = 4096
D = 1024
T = 128
NT = S_LEN // T
C0 = 0.6065306597126334
N_DMA_SEM = 8
SAME_ENGINE_SYNC = True

VEC_NAMES = []
for _i in range(2):
    VEC_NAMES += ["a_ln_g%d" % _i] + ["a_mu%d_%d" % (_i, n) for n in range(6)]
    VEC_NAMES += [s + str(_i) for s in ("a_w0", "a_a0", "a_k_k", "a_k_a", "a_r_k", "a_gn_g", "a_gn_b", "b_ln_g")]
VEC_NAMES += ["a_v0", "kv_ln_g", "final_ln_g"] + ["ple_g%d" % i for i in range(4)]
VIDX = {n: i for i, n in enumerate(VEC_NAMES)}
NVEC = len(VEC_NAMES)


class Buf:
    __slots__ = ("name", "last_write", "readers", "excl")

    def __init__(self, name=""):
        self.name = name
        self.last_write = None
        self.readers = {}
        self.excl = False


class TT_:
    def __init__(self, h, name=""):
        self.h = h
        self.b = Buf(name)
        self.b_tw = 0.0
        self.b_tr = 0.0

    def __getitem__(self, k):
        return self.h[k]


class Sch:
    def __init__(self, nc):
        self.nc = nc
        self.engs = {"pe": nc.tensor, "act": nc.scalar, "dve": nc.vector,
                     "pool": nc.gpsimd, "sp": nc.sync}
        self.sems = {}
        self.cnt = {}
        self.ops = {e: [] for e in self.engs}
        self.seen = {e: {} for e in self.engs}
        for e in self.engs:
            self.sems[e] = nc.alloc_semaphore("s_" + e)
            self.cnt[e] = 0
        self.dma_uses = {}
        self.dma_rr = {}
        for q in ("sp", "pool", "act"):
            for i in range(N_DMA_SEM):
                key = "d_%s%d" % (q, i)
                self.sems[key] = nc.alloc_semaphore(key)
                self.dma_uses[key] = 0
            self.dma_rr[q] = 0

    def _need(self, eng, waits, ev):
        if ev is None:
            return
        key, val, src = ev
        if src == eng and (eng in ("pe", "sp") or (not SAME_ENGINE_SYNC and eng != "pool")):
            return
        if self.seen[eng].get(key, 0) >= val:
            return
        if waits.get(key, 0) < val:
            waits[key] = val

    def _deps(self, eng, reads, writes):
        waits = {}
        for b in reads:
            self._need(eng, waits, b.last_write)
        for b in writes:
            self._need(eng, waits, b.last_write)
            for key, (val, src) in b.readers.items():
                self._need(eng, waits, (key, val, src))
        for key, val in waits.items():
            self.seen[eng][key] = val
        return waits

    @staticmethod
    def _mark(ev, reads, writes):
        key, val, src = ev
        for b in reads:
            b.readers[key] = (val, src)
        for b in writes:
            b.last_write = ev
            b.readers = {}

    def op(self, eng, fn, reads=(), writes=()):
        writes = [x.b for x in writes] + [x.b for x in reads if x.b.excl]
        reads = [x.b for x in reads if not x.b.excl]
        waits = self._deps(eng, reads, writes)
        self.cnt[eng] += 1
        ev = (eng, self.cnt[eng], eng)
        self._mark(ev, reads, writes)
        self.ops[eng].append((list(waits.items()), fn, eng, 1))

    def dma(self, q, fn, reads=(), writes=()):
        reads = [x.b for x in reads]
        writes = [x.b for x in writes]
        i = self.dma_rr[q]
        self.dma_rr[q] = (i + 1) % N_DMA_SEM
        key = "d_%s%d" % (q, i)
        waits = self._deps(q, reads, writes)
        prev = 16 * self.dma_uses[key]
        if prev and self.seen[q].get(key, 0) < prev:
            waits[key] = max(waits.get(key, 0), prev)
            self.seen[q][key] = prev
        self.dma_uses[key] += 1
        ev = (key, 16 * self.dma_uses[key], "dma")
        self._mark(ev, reads, writes)
        self.ops[q].append((list(waits.items()), fn, key, 16))

    def barrier(self):
        targets = {}
        for e in self.engs:
            if self.cnt[e]:
                targets[e] = self.cnt[e]
        for key, n in self.dma_uses.items():
            if n:
                targets[key] = 16 * n
        for e in self.engs:
            waits = []
            for key, val in targets.items():
                if key == e and (e in ("pe", "sp") or not SAME_ENGINE_SYNC):
                    continue
                if self.seen[e].get(key, 0) < val:
                    waits.append((key, val))
                    self.seen[e][key] = val
            if waits:
                self.ops[e].append((waits, None, None, 0))

    def emit(self):
        nc = self.nc
        sems = self.sems

        def run(name):
            def body(eng):
                for waits, fn, inc_key, inc in self.ops[name]:
                    for key, val in waits:
                        eng.wait_ge(sems[key], val)
                    if fn is not None:
                        ins = fn(eng)
                        ins.then_inc(sems[inc_key], inc)
            return body

        with nc.Block() as block:
            block.tensor(run("pe"))
            block.scalar(run("act"))
            block.vector(run("dve"))
            block.gpsimd(run("pool"))
            block.sync(run("sp"))


class K:
    def __init__(self, nc):
        self.nc = nc
        self.s = Sch(nc)
        self.sb_base = nc.sbuf_base + 64
        self.sb_base += (-self.sb_base) % 64
        self.sb_off = self.sb_base
        self.sb_top = nc.sbuf_top
        self.uid = 0
        self.psu = [TT_(nc.alloc_psum_tensor("psu%d" % i, [128, 512], F32), "psu%d" % i) for i in range(8)]
        for p_ in self.psu:
            p_.b.excl = True
        self.ps_units = tuple(range(8))
        self.ps_rrs = {}
        self.rr = 0
        self.rec = None
        self.t_avail = {}
        self.last_n = 0

    def sb(self, shape, dt, name="t"):
        nbytes = int(np.prod(shape[1:])) * (4 if dt == F32 else 2)
        nbytes += (-nbytes) % 64
        assert self.sb_off + nbytes <= self.sb_top, ("SBUF overflow", name, self.sb_off, nbytes)
        self.uid += 1
        h = self.nc.alloc_sbuf_tensor_at("%s_%d" % (name, self.uid), list(shape), dt, offset=self.sb_off)
        self.sb_off += nbytes
        return TT_(h, name)

    def mark(self):
        return self.sb_off

    def release(self, m):
        self.s.barrier()
        self.sb_off = m

    def ps(self):
        units = self.ps_units
        i = self.ps_rrs.get(units, 0)
        self.ps_rrs[units] = (i + 1) % len(units)
        return self.psu[units[i]]

    def alt(self, engs=("dve", "pool")):
        self.rr += 1
        return engs[self.rr % len(engs)]

    def _op(self, eng, fn, R, W):
        if self.rec is not None:
            n = self.last_n if eng != "dve_recip" else 6 * self.last_n
            self.rec.append((0, eng, fn, list(R), list(W), n))
        else:
            self.s.op(eng, fn, R, W)

    def _dma(self, q, fn, R, W):
        if self.rec is not None:
            self.rec.append((1, q, fn, list(R), list(W), 0))
        else:
            self.s.dma(q, fn, R, W)

    def record(self, fn, units):
        assert self.rec is None
        self.rec = []
        old = self.ps_units
        self.ps_units = tuple(units)
        try:
            fn()
            return self.rec
        finally:
            self.rec = None
            self.ps_units = old

    def _cost(self, kind, eng, n):
        if kind == 1:
            return 0.1, 3.0
        if eng == "pe":
            d = 0.035 + 0.00045 * n
        elif eng == "act":
            d = 0.2 + 0.00095 * n
        elif eng == "dve":
            d = 0.1 + 0.0011 * n
        else:
            d = 0.2 + 0.0022 * n
        return d, d

    def merge(self, streams):
        pos = [0] * len(streams)
        total = sum(len(x) for x in streams)
        avail = self.t_avail
        for _ in range(total):
            best, bt, bfin = -1, None, None
            for i, st in enumerate(streams):
                if pos[i] >= len(st):
                    continue
                kind, e, fn, R, W, n = st[pos[i]]
                t0 = avail.get(e, 0.0)
                for x in R:
                    t0 = max(t0, x.b_tw)
                for x in W:
                    t0 = max(t0, x.b_tw, x.b_tr)
                key = (t0, pos[i] / len(st))
                if bt is None or key < bt:
                    best, bt = i, key
            kind, e, fn, R, W, n = streams[best][pos[best]]
            pos[best] += 1
            t0 = bt[0]
            busy, lat = self._cost(kind, e, n)
            avail[e] = t0 + busy
            fin = t0 + lat
            for x in R:
                x.b_tr = max(x.b_tr, fin)
            for x in W:
                x.b_tw = fin
                x.b_tr = 0.0
            if kind == 0:
                self.s.op(e, fn, R, W)
            else:
                self.s.dma(e, fn, R, W)

    def tt(self, eng, out, a, b, op, R, W):
        self.last_n = int(np.prod(out.shape[1:]))
        self._op(eng, lambda e: e.tensor_tensor(out, a, b, op), R, W)

    def stt(self, eng, out, in0, scalar, in1, op0, op1, R, W):
        self.last_n = int(np.prod(out.shape[1:]))
        self._op(eng, lambda e: e.scalar_tensor_tensor(out, in0, scalar, in1, op0, op1), R, W)

    def ts(self, eng, out, in0, s1, s2, op0, op1, R, W):
        self.last_n = int(np.prod(out.shape[1:]))
        if op1 is None:
            self._op(eng, lambda e: e.tensor_scalar(out, in0, s1, None, op0), R, W)
        else:
            self._op(eng, lambda e: e.tensor_scalar(out, in0, s1, s2, op0, op1), R, W)

    def act(self, out, in_, func, R, W, bias=None, scale=1.0):
        self.last_n = int(np.prod(out.shape[1:]))
        if bias is None:
            self._op("act", lambda e: e.activation(out, in_, func, scale=scale), R, W)
        else:
            self._op("act", lambda e: e.activation(out, in_, func, bias=bias, scale=scale), R, W)

    def cp(self, eng, out, in_, R, W):
        self.last_n = int(np.prod(out.shape[1:]))
        if eng == "act":
            self._op("act", lambda e: e.copy(out, in_), R, W)
        else:
            self._op(eng, lambda e: e.tensor_copy(out, in_), R, W)

    def memset(self, eng, out, val, W):
        self.last_n = int(np.prod(out.shape[1:]))
        self._op(eng, lambda e: e.memset(out, val), [], W)

    def recip(self, out, in_, R, W):
        self.last_n = 6 * int(np.prod(out.shape[1:]))
        self._op("dve", lambda e: e.reciprocal(out, in_), R, W)

    def scan(self, out, d0, d1, R, W):
        self.last_n = int(np.prod(out.shape[1:]))
        self._op("dve", lambda e: e.tensor_tensor_scan(out, d0, d1, 0.0, ALU.mult, ALU.add), R, W)

    def mm(self, out, lhsT, rhs, start, stop, R, W, tp=None):
        self.last_n = int(np.prod(out.shape[1:]))
        if tp is None:
            self._op("pe", lambda e: e.matmul(out, lhsT, rhs, start=start, stop=stop), R, W)
        else:
            self._op("pe", lambda e: e.matmul(out, lhsT, rhs, start=start, stop=stop, tile_position=tp), R, W)

    def tr(self, out, in_, ident, R, W):
        self.last_n = int(np.prod(out.shape[1:]))
        self._op("pe", lambda e: e.transpose(out, in_, ident), R, W)

    def dma(self, q, out, in_, R, W):
        self._dma(q, lambda e: e.dma_start(out=out, in_=in_), R, W)


def bc(ap, shape):
    return ap.to_broadcast(list(shape))


def build_program(stop_after=None):
    nc = bass.Bass("TRN2", target_bir_lowering=False)
    k = K(nc)

    def din(name, shape, dt=F32):
        return nc.dram_tensor(name, list(shape), dt, kind="ExternalInput").ap()

    def dscr(name, shape, dt):
        return TT_(nc.dram_tensor(name, list(shape), dt, kind="Internal").ap(), name)

    xT = TT_(din("xT", [128, 8, S_LEN]), "xT")
    pT = TT_(din("pT", [4, 128, 2, S_LEN]), "pT")
    W = {}
    W["rkvg"] = din("a_w_rkvg", [2, 4, 128, 8, 1024])
    W["w1"] = din("a_w1", [2, 128, 8, 64])
    W["w2"] = din("a_w2", [2, 64, 1024])
    W["a1"] = din("a_a1", [2, 128, 8, 64])
    W["a2"] = din("a_a2", [2, 64, 1024])
    W["v1"] = din("a_v1", [1, 128, 8, 32])
    W["v2"] = din("a_v2", [1, 32, 1024])
    W["wo"] = din("a_w_o", [2, 128, 8, 1024])
    W["ple_w"] = din("ple_w", [4, 128, 2, 1024])
    W["ple_gate"] = din("ple_w_gate", [4, 128, 8, 1024])
    W["w_kv"] = din("w_kv", [128, 8, 2048])
    W["b_w_in"] = din("b_w_in", [2, 128, 8, 4096])
    W["b_w_o"] = din("b_w_o", [2, 128, 8, 1024])
    pvec_d = din("pvec", [128, NVEC * 8])
    cmask_d = din("cmask", [128, 4, 128])
    amask_d = din("amask", [128, 8, 256])
    outT = TT_(nc.dram_tensor("outT", [128, 8, S_LEN], F32, kind="ExternalOutput").ap(), "outT")

    hT = dscr("hT", [128, 8, S_LEN], F32)
    vfT = dscr("vfT", [NT, 128, 8 * T], F32)
    pkS = dscr("pkS", [NT, 128, 8 * 7 * T], BF16)
    edS = dscr("edS", [NT, 128, 16], F32)
    dram_w = TT_(None, "dram_w")

    pv = k.sb([128, NVEC * 8], F32, "pvec")
    k.dma("sp", pv[:], pvec_d, [], [pv])

    def vcol(name):
        i = VIDX[name]
        return pv[:, i * 8:(i + 1) * 8]

    cm32 = k.sb([128, 4, 128], F32, "cm32")
    k.dma("sp", cm32[:], cmask_d, [], [cm32])
    identb = k.sb([128, 128], BF16, "identb")
    k.cp("dve", identb[:], cm32[:, 0, :], [cm32], [identb])
    mask2 = k.sb([128, 256], BF16, "mask2")
    k.cp("dve", mask2[:, 0:128], cm32[:, 1, :], [cm32], [mask2])
    k.cp("dve", mask2[:, 128:256], cm32[:, 2, :], [cm32], [mask2])
    masklo = k.sb([128, 128], BF16, "masklo")
    k.cp("dve", masklo[:], cm32[:, 3, :], [cm32], [masklo])
    onesN = k.sb([128, 128], BF16, "onesN")
    k.memset("dve", onesN[:], 1.0 / 1024.0, [onesN])
    blk1 = k.sb([128, 128], BF16, "blk1")
    blk64 = k.sb([128, 128], BF16, "blk64")
    k.memset("dve", blk1[:], 0.0, [blk1])
    k.memset("dve", blk64[:], 0.0, [blk64])
    for hh in range(2):
        pr = slice(hh * 64, hh * 64 + 64)
        k.memset("dve", blk1[pr, pr], 1.0, [blk1])
        k.memset("dve", blk64[pr, pr], 1.0 / 64.0, [blk64])
    onesT = k.sb([128, T], F32, "onesT")
    k.memset("dve", onesT[:], 1.0, [onesT])
    epsN = k.sb([128, 1], F32, "epsN")
    k.memset("dve", epsN[:], 1e-6, [epsN])
    epsG = k.sb([128, 1], F32, "epsG")
    k.memset("dve", epsG[:], 64e-5, [epsG])
    stage = []
    stage_rr = [0]

    def with_stage(fn):
        m_ = k.mark()
        stage[:] = [k.sb([128, 1024], F32, "stage%d" % i) for i in range(3)]
        fn()
        k.release(m_)
        stage[:] = []

    def load_w(dst, dst_ap_fn, src_ap_fn, nk, ncols):
        for kc in range(nk):
            for c0 in range(0, ncols, 1024):
                cw = min(1024, ncols - c0)
                st = stage[stage_rr[0] % 3]
                stage_rr[0] += 1
                k.dma("sp", st[:, 0:cw], src_ap_fn(kc, c0, cw), [], [st])
                k.cp(k.alt(("act", "pool", "dve")), dst_ap_fn(kc, c0, cw), st[:, 0:cw], [st], [dst])

    def rmsnorm(src32, gname, out_ap, outT_, tmp32, sqb, rstd):
        k.tt("pool", tmp32[:], src32[:], bc(vcol(gname).unsqueeze(2), [128, 8, T]), ALU.mult, [src32, pv], [tmp32])
        k.act(sqb[:], src32[:], AF.Square, [src32], [sqb])
        p = k.ps()
        for c in range(8):
            k.mm(p[:, 0:T], onesN[:], sqb[:, c, :], c == 0, c == 7, [onesN, sqb], [p])
        k.act(rstd[:], p[:, 0:T], AF.Sqrt, [p, epsN], [rstd], bias=epsN[:, 0:1])
        k.recip(rstd[:], rstd[:], [rstd], [rstd])
        k.tt("dve", out_ap, tmp32[:], bc(rstd[:].unsqueeze(1), [128, 8, T]), ALU.mult, [tmp32, rstd], [outT_])

    base_mark = k.mark()

    def rwkv_layer(li, src):
        m0 = k.mark()
        Wr = [k.sb([128, 8, 1024], BF16, "Wrkvg%d" % n) for n in range(4)]
        w1 = k.sb([128, 8, 64], BF16, "w1")
        a1 = k.sb([128, 8, 64], BF16, "a1")
        w2 = k.sb([64, 1024], BF16, "w2")
        a2 = k.sb([64, 1024], BF16, "a2")
        if li == 1:
            v1 = k.sb([128, 8, 32], BF16, "v1")
            v2 = k.sb([32, 1024], BF16, "v2")

        def load_A():
            for n in range(4):
                load_w(Wr[n], lambda kc, c0, cw, n=n: Wr[n][:, kc, c0:c0 + cw],
                       lambda kc, c0, cw, n=n: W["rkvg"][li, n, :, kc, c0:c0 + cw], 8, 1024)
            for (dst, key) in ((w1, "w1"), (a1, "a1")):
                st = stage[stage_rr[0] % 3]; stage_rr[0] += 1
                k.dma("sp", st[:, 0:512], W[key][li].rearrange("p c n -> p (c n)"), [], [st])
                k.cp("dve", dst[:].rearrange("p c n -> p (c n)"), st[:, 0:512], [st], [dst])
            for (dst, key) in ((w2, "w2"), (a2, "a2")):
                st = stage[stage_rr[0] % 3]; stage_rr[0] += 1
                k.dma("sp", st[0:64, :], W[key][li], [], [st])
                k.cp("dve", dst[:], st[0:64, :], [st], [dst])
            if li == 1:
                st = stage[stage_rr[0] % 3]; stage_rr[0] += 1
                k.dma("sp", st[:, 0:256], W["v1"][0].rearrange("p c n -> p (c n)"), [], [st])
                k.cp("dve", v1[:].rearrange("p c n -> p (c n)"), st[:, 0:256], [st], [v1])
                st = stage[stage_rr[0] % 3]; stage_rr[0] += 1
                k.dma("sp", st[0:32, :], W["v2"][0], [], [st])
                k.cp("dve", v2[:], st[0:32, :], [st], [v2])
        with_stage(load_A)

        L = str(li)
        h32 = k.sb([128, 8, T], F32, "h32")
        sqb = k.sb([128, 8, T], BF16, "sqb")
        rstd = k.sb([128, T], F32, "rstd")
        xnb = [k.sb([128, 8, T + 1], F32, "xnb%d" % i) for i in range(2)]
        xx = k.sb([128, 8, T], F32, "xx")
        tmpA = k.sb([128, 8, T], F32, "tmpA")
        xm = [k.sb([128, 8, T], BF16, "xm%d" % n) for n in range(6)]
        XB = []
        for sl_ in range(2):
            d_ = dict(r32=k.sb([128, 8, T], F32, "r32"), k32=k.sb([128, 8, T], F32, "k32"), v32=k.sb([128, 8, T], F32, "v32"),
                      sig=k.sb([128, 8, T], F32, "sig"), a32=k.sb([128, 8, T], F32, "a32"), gsb=k.sb([128, 8, T], BF16, "gsb"))
            if li == 1:
                d_["vm"] = k.sb([128, 8, T], F32, "vm")
            XB.append(d_)
        sqb2 = k.sb([128, 8, T], BF16, "sqb2")
        kk = k.sb([128, 8, T], F32, "kk")
        rn = k.sb([128, 8, T], F32, "rn")
        bb = k.sb([128, 8, T], F32, "bb")
        cs = k.sb([128, 8, T], F32, "cs")
        dd = k.sb([128, 8, T], F32, "dd")
        e_in = k.sb([128, 8, T], F32, "e_in")
        e_out = k.sb([128, 8, T], F32, "e_out")
        thb = k.sb([64, T], BF16, "thb")
        pk = k.sb([128, 8, 6, T], BF16, "pk")
        ed = k.sb([128, 16], F32, "ed")
        if li == 1:
            vf = dd
        k.memset("dve", xnb[0][:, :, 0:1], 0.0, [xnb[0]])

        def lora(xin, wA, wB, nA, func_mid, bias_name, out32):
            p = k.ps()
            for kc in range(8):
                k.mm(p[0:nA, 0:T], wA[:, kc, :], xin[:, kc, :], kc == 0, kc == 7, [wA, xin], [p])
            k.act(thb[0:nA, :], p[0:nA, 0:T], func_mid, [p], [thb])
            for hf in range(2):
                p2 = k.ps()
                for j in range(4):
                    oc = hf * 4 + j
                    k.mm(p2[:, j * T:(j + 1) * T], wB[:, oc * 128:(oc + 1) * 128], thb[0:nA, :], True, True, [wB, thb], [p2])
                cs_ = slice(hf * 4, hf * 4 + 4)
                k.tt("dve", tmpA[:, cs_, :], p2[:].rearrange("p (c t) -> p c t", c=4),
                     bc(vcol(bias_name)[:, cs_].unsqueeze(2), [128, 4, T]), ALU.add, [p2, pv], [tmpA])
                k.act(out32[:, cs_, :], tmpA[:, cs_, :], AF.Sigmoid, [tmpA], [out32])

        def frontA(t, sl):
            X_ = XB[sl]
            r32, k32, v32, sig, a32, gsb = (X_[x] for x in ("r32", "k32", "v32", "sig", "a32", "gsb"))
            t0 = t * T
            par = t % 2
            xn = xnb[par]
            k.dma("sp", h32[:], src[:, :, t0:t0 + T], [src], [h32])
            rmsnorm(h32, "a_ln_g" + L, xn[:, :, 1:T + 1], xn, tmpA, sqb, rstd)
            k.tt("pool", xx[:], xn[:, :, 0:T], xn[:, :, 1:T + 1], ALU.subtract, [xn], [xx])
            k.cp("pool", xnb[1 - par][:, :, 0:1], xn[:, :, T:T + 1], [xn], [xnb[1 - par]])
            for n in range(6):
                e1 = k.alt()
                tmp = tmpA if n % 2 == 0 else h32
                k.tt(e1, tmp[:], xx[:], bc(vcol("a_mu%d_%d" % (li, n)).unsqueeze(2), [128, 8, T]), ALU.mult, [xx, pv], [tmp])
                k.tt(e1, xm[n][:], tmp[:], xn[:, :, 1:T + 1], ALU.add, [tmp, xn], [xm[n]])
            for n in range(4):
                for hf in range(2):
                    p = k.ps()
                    for j in range(4):
                        oc = hf * 4 + j
                        for kc in range(8):
                            k.mm(p[:, j * T:(j + 1) * T], Wr[n][:, kc, oc * 128:(oc + 1) * 128], xm[n][:, kc, :],
                                 kc == 0, kc == 7, [Wr[n], xm[n]], [p])
                    cs_ = slice(hf * 4, hf * 4 + 4)
                    pv3 = p[:].rearrange("p (c t) -> p c t", c=4)
                    if n == 0:
                        k.cp("act", r32[:, cs_, :], pv3, [p], [r32])
                    elif n == 1:
                        k.cp("dve", k32[:, cs_, :], pv3, [p], [k32])
                    elif n == 2:
                        k.cp("act", v32[:, cs_, :], pv3, [p], [v32])
                    else:
                        k.act(gsb[:, cs_, :], pv3, AF.Silu, [p], [gsb])
            lora(xm[4], w1, w2, 64, AF.Tanh, "a_w0" + L, sig)
            lora(xm[5], a1, a2, 64, AF.Copy, "a_a0" + L, a32)
            if li == 1:
                lora(xm[2], v1, v2, 32, AF.Copy, "a_v0", X_["vm"])

        def backA(t, sl):
            X_ = XB[sl]
            r32, k32, v32, sig, a32, gsb = (X_[x] for x in ("r32", "k32", "v32", "sig", "a32", "gsb"))
            if li == 0:
                k.dma("pool", vfT[t].rearrange("p (c t) -> p c t", c=8), v32[:], [v32], [vfT])
            else:
                vm = X_["vm"]
                k.dma("sp", vf[:], vfT[t].rearrange("p (c t) -> p c t", c=8), [vfT], [vf])
                k.tt("dve", vf[:], vf[:], v32[:], ALU.subtract, [vf, v32], [vf])
                k.tt("dve", vf[:], vf[:], vm[:], ALU.mult, [vf, vm], [vf])
                k.tt("dve", v32[:], v32[:], vf[:], ALU.add, [v32, vf], [v32])
            k.tt("pool", kk[:], k32[:], bc(vcol("a_k_k" + L).unsqueeze(2), [128, 8, T]), ALU.mult, [k32, pv], [kk])
            k.act(sqb2[:], kk[:], AF.Square, [kk], [sqb2])
            for hf in range(2):
                p = k.ps()
                for j in range(4):
                    k.mm(p[:, j * T:(j + 1) * T], blk1[:], sqb2[:, hf * 4 + j, :], True, True, [blk1, sqb2], [p])
                k.act(rn[:, hf * 4:hf * 4 + 4, :], p[:].rearrange("p (c t) -> p c t", c=4), AF.Sqrt, [p], [rn])
            k.ts("dve", rn[:], rn[:], 1e-12, None, ALU.max, None, [rn], [rn])
            k.recip(rn[:], rn[:], [rn], [rn])
            k.tt("pool", kk[:], kk[:], rn[:], ALU.mult, [kk, rn], [kk])
            k.tt("pool", bb[:], kk[:], a32[:], ALU.mult, [kk, a32], [bb])
            k.stt("dve", a32[:], a32[:], -1.0, bc(vcol("a_k_a" + L).unsqueeze(2), [128, 8, T]), ALU.add, ALU.mult, [a32, pv], [a32])
            k.stt("dve", k32[:], a32[:], 1.0, k32[:], ALU.add, ALU.mult, [a32, k32], [k32])
            for c in range(8):
                k.scan(cs[:, c, :], onesT[:], sig[:, c, :], [onesT, sig], [cs])
            k.tt("pool", dd[:], cs[:], bc(cs[:, :, T // 2 - 1:T // 2], [128, 8, T]), ALU.subtract, [cs], [dd])
            k.act(e_in[:], dd[:], AF.Exp, [dd], [e_in], scale=-C0)
            k.act(e_out[:], dd[:], AF.Exp, [dd], [e_out], scale=C0)
            k.tt("pool", rn[:], dd[:], sig[:], ALU.subtract, [dd, sig], [rn])
            k.act(rn[:], rn[:], AF.Exp, [rn], [rn], scale=-C0)
            k.act(ed[:, 0:8], cs[:, :, T // 2 - 1], AF.Exp, [cs], [ed], scale=-C0)
            k.cp("dve", ed[:, 8:16], e_in[:, :, T - 1], [e_in], [ed])
            k.stt("dve", pk[:, :, 0, :], kk[:], -1.0, rn[:], ALU.mult, ALU.mult, [kk, rn], [pk])
            k.tt("pool", pk[:, :, 1, :], r32[:], e_in[:], ALU.mult, [r32, e_in], [pk])
            k.tt("dve", pk[:, :, 2, :], bb[:], e_out[:], ALU.mult, [bb, e_out], [pk])
            k.tt("pool", pk[:, :, 3, :], k32[:], e_out[:], ALU.mult, [k32, e_out], [pk])
            k.cp("act", pk[:, :, 4, :], v32[:], [v32], [pk])
            k.tt("dve", bb[:], r32[:], k32[:], ALU.mult, [r32, k32], [bb])
            k.tt("pool", pk[:, :, 5, :], bb[:], bc(vcol("a_r_k" + L).unsqueeze(2), [128, 8, T]), ALU.mult, [bb, pv], [pk])
            pkS_t = pkS[t].rearrange("p (c q t) -> p c q t", c=8, q=7)
            k.dma("pool", pkS_t[:, :, 0:6, :], pk[:], [pk], [pkS])
            k.dma("pool", pkS_t[:, :, 6, :], gsb[:], [gsb], [pkS])
            k.dma("pool", edS[t], ed[:], [ed], [edS])

        frontA(0, 0)
        for t in range(NT):
            streams = [k.record(lambda: backA(t, t % 2), range(0, 4))]
            if t + 1 < NT:
                streams.append(k.record(lambda: frontA(t + 1, (t + 1) % 2), range(4, 8)))
            k.merge(streams)
        k.release(m0)
        if stop_after == "A":
            return

        Wo = k.sb([128, 8, 1024], BF16, "Wo")
        ple_load, ple_alloc, finish_layer = make_ple(li, 1)

        def load_B():
            load_w(Wo, lambda kc, c0, cw: Wo[:, kc, c0:c0 + cw], lambda kc, c0, cw: W["wo"][li, :, kc, c0:c0 + cw], 8, 1024)
            ple_load()
        with_stage(load_B)
        pqB = k.sb([128, 8, 3, T], BF16, "pqB")
        pk = k.sb([128, 8, 5, T], BF16, "pkB")
        h32 = k.sb([128, 8, T], F32, "h32B")
        LT = k.sb([128, 16, 128], BF16, "LT")
        Pb = [k.sb([128, 16, 128], BF16, "Pb%d" % i) for i in range(2)]
        PTb = [k.sb([128, 16, 128], BF16, "PTb%d" % i) for i in range(2)]
        Tb = [k.sb([128, 16, 128], BF16, "Tb%d" % i) for i in range(2)]
        XTs = k.sb([128, 1024], BF16, "XTs")
        UTs = k.sb([128, 1024], BF16, "UTs")
        S32 = k.sb([128, 8, 64], F32, "S32")
        S0m32 = k.sb([128, 8, 64], F32, "S0m32")
        y32 = k.sb([128, 8, T], F32, "y32")
        yb = k.sb([128, 8, T], BF16, "yb")
        ysq = k.sb([128, 8, T], BF16, "ysq")
        mean32 = k.sb([128, 8, T], F32, "mean32")
        var32 = k.sb([128, 8, T], F32, "var32")
        zb = k.sb([128, 8, T], BF16, "zb")
        h1 = y32
        ple_alloc(dict(tmp=var32, sg=mean32, ho=h32))
        k.memset("dve", S32[:], 0.0, [S32])
        UTz = k.sb([128, 8, 2, 2, 64], BF16, "UTz")
        S0bd = k.sb([128, 8, 2, 64], BF16, "S0bd")
        for z_ in (UTz, S0bd):
            k.memset("pool", z_[:], 0.0, [z_])
        PBs = []
        for sl in range(2):
            d_ = dict(ed=k.sb([128, 16], F32, "edB"), tok=[k.sb([128, 1024], BF16, "tok%d" % i) for i in range(3)],
                      LBs=k.sb([128, 16, 256], BF16, "LBs"), LKs=k.sb([128, 16, 256], BF16, "LKs"),
                      Tf=k.sb([128, 16, 128], BF16, "Tf"), ARz=k.sb([128, 8, 2, 2, T], BF16, "ARz"),
                      VTz=k.sb([128, 8, 2, 2, 64], BF16, "VTz"))
            for z_ in (d_["ARz"], d_["VTz"]):
                k.memset("pool", z_[:], 0.0, [z_])
            PBs.append(d_)

        def pre(t, sl):
            P_ = PBs[sl]
            ed, tok, LBs, LKs, Tf, ARz, VTz = (P_[x] for x in ("ed", "tok", "LBs", "LKs", "Tf", "ARz", "VTz"))
            BTt, KTt, VT = tok
            k.dma("sp", pk[:], pkS[t].rearrange("p (c q t) -> p c q t", c=8, q=7)[:, :, 0:5, :], [pkS], [pk])
            k.dma("sp", ed[:], edS[t], [edS], [ed])
            for qi, q in enumerate((2, 3, 4)):
                p = k.ps()
                pb = p[:].bitcast(BF16)
                for c in range(8):
                    k.tr(pb[:, c * 128:(c + 1) * 128], pk[:, c, q, :], identb[:], [pk, identb], [p])
                k.cp("act" if qi != 1 else "dve", tok[qi][:], pb, [p], [tok[qi]])
            for hh in range(2):
                pr = slice(hh * 64, hh * 64 + 64)
                k.cp("pool", ARz[pr, :, hh, :, :], pk[pr, :, 0:2, :], [pk], [ARz])
                k.cp("pool", VTz[:, :, hh, hh, :], VT[:].rearrange("p (c h i) -> p c h i", c=8, h=2)[:, :, hh, :], [VT], [VTz])
            for c in range(8):
                pLB = k.ps(); pLK = k.ps(); pLT = k.ps()
                for hh in range(2):
                    rhsAR = ARz[:, c, hh, :, :].rearrange("p a t -> p (a t)")
                    k.mm(pLB[:, hh * 256:(hh + 1) * 256], pk[:, c, 2, :], rhsAR, True, True, [pk, ARz], [pLB])
                    k.mm(pLK[:, hh * 256:(hh + 1) * 256], pk[:, c, 3, :], rhsAR, True, True, [pk, ARz], [pLK])
                    k.mm(pLT[:, hh * 128:(hh + 1) * 128], ARz[:, c, hh, 0, :], pk[:, c, 2, :], True, True, [pk, ARz], [pLT])
                k.tt("dve", LBs[:, 2 * c:2 * c + 2, :], pLB[:].rearrange("p (h x) -> p h x", h=2),
                     bc(mask2[:].unsqueeze(1), [128, 2, 256]), ALU.mult, [pLB, mask2], [LBs])
                k.tt("dve", LKs[:, 2 * c:2 * c + 2, :], pLK[:].rearrange("p (h x) -> p h x", h=2),
                     bc(mask2[:].unsqueeze(1), [128, 2, 256]), ALU.mult, [pLK, mask2], [LKs])
                k.tt("dve", LT[:, 2 * c:2 * c + 2, :], pLT[:, 0:256].rearrange("p (h x) -> p h x", h=2),
                     bc(masklo[:].unsqueeze(1), [128, 2, 128]), ALU.mult, [pLT, masklo], [LT])
            k.tt("pool", Tb[0][:], LBs[:, :, 0:128], bc(identb[:].unsqueeze(1), [128, 16, 128]), ALU.add, [LBs, identb], [Tb[0]])
            Pc_ap = lambda h: LBs[:, h, 0:128]
            PTc_ap = lambda h: LT[:, h, :]
            Pc_t, PTc_t = LBs, LT
            Tc = 0
            for lvl in range(1, 7):
                Pn, PTn = Pb[lvl % 2], PTb[lvl % 2]
                for g in range(4):
                    p1 = k.ps(); p2 = k.ps()
                    for j in range(4):
                        h = g * 4 + j
                        k.mm(p1[:, j * 128:(j + 1) * 128], PTc_ap(h), Pc_ap(h), True, True, [Pc_t, PTc_t], [p1])
                    for j in range(4):
                        h = g * 4 + j
                        k.mm(p2[:, j * 128:(j + 1) * 128], Pc_ap(h), PTc_ap(h), True, True, [Pc_t, PTc_t], [p2])
                    k.cp("act", Pn[:, g * 4:g * 4 + 4, :], p1[:].rearrange("p (h x) -> p h x", h=4), [p1], [Pn])
                    k.cp("act", PTn[:, g * 4:g * 4 + 4, :], p2[:].rearrange("p (h x) -> p h x", h=4), [p2], [PTn])
                Told = Tb[Tc]
                Tnew = Tb[1 - Tc] if lvl < 6 else Tf
                for g in range(4):
                    p3 = k.ps()
                    for j in range(4):
                        h = g * 4 + j
                        k.mm(p3[:, j * 128:(j + 1) * 128], PTn[:, h, :], Told[:, h, :], True, True, [PTn, Told], [p3])
                    k.tt("dve", Tnew[:, g * 4:g * 4 + 4, :], p3[:].rearrange("p (h x) -> p h x", h=4),
                         Told[:, g * 4:g * 4 + 4, :], ALU.add, [p3, Told], [Tnew])
                Tc = 1 - Tc
                Pc_t, PTc_t = Pn, PTn
                Pc_ap = lambda h, Pn=Pn: Pn[:, h, :]
                PTc_ap = lambda h, PTn=PTn: PTn[:, h, :]

        def chain(t, sl):
            P_ = PBs[sl]
            ed, tok, LBs, LKs, Tfin, ARz, VTz = (P_[x] for x in ("ed", "tok", "LBs", "LKs", "Tf", "ARz", "VTz"))
            BTt, KTt, VT = tok
            k.tt("dve", S0m32[:], S32[:], bc(ed[:, 0:8].unsqueeze(2), [128, 8, 64]), ALU.mult, [S32, ed], [S0m32])
            for hh in range(2):
                pr = slice(hh * 64, hh * 64 + 64)
                k.cp("act", S0bd[pr, :, hh, :], S0m32[pr, :, :], [S0m32], [S0bd])
            for u in range(2):
                p = k.ps()
                for j in range(8):
                    h = u * 8 + j
                    c, hh = h // 2, h % 2
                    k.mm(p[:, j * 64:(j + 1) * 64], ARz[:, c, hh, 0, :], S0bd[:, c, hh, :], True, False, [ARz, S0bd], [p])
                    k.mm(p[:, j * 64:(j + 1) * 64], LKs[:, h, 0:128], VT[:, h * 64:(h + 1) * 64], False, True, [LKs, VT], [p])
                k.cp("act", XTs[:, u * 512:(u + 1) * 512], p[:], [p], [XTs])
            for u in range(2):
                p = k.ps()
                for j in range(8):
                    h = u * 8 + j
                    k.mm(p[:, j * 64:(j + 1) * 64], Tfin[:, h, :], XTs[:, h * 64:(h + 1) * 64], True, True, [Tfin, XTs], [p])
                k.cp("act", UTs[:, u * 512:(u + 1) * 512], p[:], [p], [UTs])
                p4 = p[:].rearrange("p (c h i) -> p c h i", c=4, h=2)
                for hh in range(2):
                    k.cp("dve", UTz[:, 4 * u:4 * u + 4, hh, hh, :], p4[:, :, hh, :], [p], [UTz])
            pS = []
            for u in range(2):
                p = k.ps()
                for j in range(8):
                    h = u * 8 + j
                    c = h // 2
                    k.mm(p[:, j * 64:(j + 1) * 64], BTt[:, c * 128:(c + 1) * 128], UTs[:, h * 64:(h + 1) * 64], True, False, [BTt, UTs], [p])
                    k.mm(p[:, j * 64:(j + 1) * 64], KTt[:, c * 128:(c + 1) * 128], VT[:, h * 64:(h + 1) * 64], False, True, [KTt, VT], [p])
                pS.append(p)
            for u in range(2):
                p4 = pS[u][:].rearrange("p (c h i) -> p c h i", c=4, h=2)
                for hh in range(2):
                    pr = slice(hh * 64, hh * 64 + 64)
                    k.tt("dve", S0m32[pr, 4 * u:4 * u + 4, :], S0m32[pr, 4 * u:4 * u + 4, :], p4[pr, :, hh, :], ALU.add, [S0m32, pS[u]], [S0m32])
            k.tt("dve", S32[:], S0m32[:], bc(ed[:, 8:16].unsqueeze(2), [128, 8, 64]), ALU.mult, [S0m32, ed], [S32])
            for hf in range(2):
                p = k.ps()
                for j in range(4):
                    c = hf * 4 + j
                    o = p[:, j * T:(j + 1) * T]
                    for hh in range(2):
                        k.mm(o, S0bd[:, c, :, :].rearrange("p h i -> p (h i)"), ARz[:, c, hh, 1, :], hh == 0, False, [S0bd, ARz], [p])
                    for hh in range(2):
                        h = 2 * c + hh
                        k.mm(o, UTz[:, c, hh, :, :].rearrange("p h i -> p (h i)"), LBs[:, h, 128:256], False, False, [UTz, LBs], [p])
                        k.mm(o, VTz[:, c, hh, :, :].rearrange("p h i -> p (h i)"), LKs[:, h, 128:256], False, hh == 1, [VTz, LKs], [p])
                cs_ = slice(hf * 4, hf * 4 + 4)
                pv3 = p[:].rearrange("p (c t) -> p c t", c=4)
                k.cp("act", y32[:, cs_, :], pv3, [p], [y32])
                k.act(ysq[:, cs_, :], pv3, AF.Square, [p], [ysq])
                k.cp("dve", yb[:, cs_, :], pv3, [p], [yb])

        def post(t):
            t0 = t * T
            k.dma("sp", pqB[:], pkS[t].rearrange("p (c q t) -> p c q t", c=8, q=7)[:, :, 4:7, :], [pkS], [pqB])
            k.dma("sp", h32[:], src[:, :, t0:t0 + T], [src], [h32])
            for hf in range(2):
                cs_ = slice(hf * 4, hf * 4 + 4)
                pm = k.ps(); pq = k.ps()
                for j in range(4):
                    k.mm(pm[:, j * T:(j + 1) * T], blk64[:], yb[:, hf * 4 + j, :], True, True, [blk64, yb], [pm])
                for j in range(4):
                    k.mm(pq[:, j * T:(j + 1) * T], blk64[:], ysq[:, hf * 4 + j, :], True, True, [blk64, ysq], [pq])
                k.cp("act", mean32[:, cs_, :], pm[:].rearrange("p (c t) -> p c t", c=4), [pm], [mean32])
                k.tt("pool", var32[:, cs_, :], mean32[:, cs_, :], mean32[:, cs_, :], ALU.mult, [mean32], [var32])
                k.tt("dve", var32[:, cs_, :], pq[:].rearrange("p (c t) -> p c t", c=4), var32[:, cs_, :], ALU.subtract, [pq, var32], [var32])
            k.ts("dve", var32[:], var32[:], 0.0, None, ALU.max, None, [var32], [var32])
            k.act(var32[:], var32[:], AF.Sqrt, [var32, epsG], [var32], bias=epsG[:, 0:1])
            k.recip(var32[:], var32[:], [var32], [var32])
            k.tt("pool", y32[:], y32[:], mean32[:], ALU.subtract, [y32, mean32], [y32])
            k.tt("dve", y32[:], y32[:], var32[:], ALU.mult, [y32, var32], [y32])
            k.tt("pool", y32[:], y32[:], bc(vcol("a_gn_g" + L).unsqueeze(2), [128, 8, T]), ALU.mult, [y32, pv], [y32])
            k.tt("pool", y32[:], y32[:], bc(vcol("a_gn_b" + L).unsqueeze(2), [128, 8, T]), ALU.add, [y32, pv], [y32])
            for hf in range(2):
                cs_ = slice(hf * 4, hf * 4 + 4)
                pr_ = k.ps()
                for j in range(4):
                    k.mm(pr_[:, j * T:(j + 1) * T], blk1[:], pqB[:, hf * 4 + j, 1, :], True, True, [blk1, pqB], [pr_])
                k.tt("dve", mean32[:, cs_, :], pr_[:].rearrange("p (c t) -> p c t", c=4), pqB[:, cs_, 0, :], ALU.mult, [pr_, pqB], [mean32])
            k.tt("pool", y32[:], y32[:], mean32[:], ALU.add, [y32, mean32], [y32])
            k.tt("dve", zb[:], y32[:], pqB[:, :, 2, :], ALU.mult, [y32, pqB], [zb])
            for hf in range(2):
                cs_ = slice(hf * 4, hf * 4 + 4)
                p = k.ps()
                for j in range(4):
                    oc = hf * 4 + j
                    for kc in range(8):
                        k.mm(p[:, j * T:(j + 1) * T], Wo[:, kc, oc * 128:(oc + 1) * 128], zb[:, kc, :], kc == 0, kc == 7, [Wo, zb], [p])
                k.tt("dve", h1[:, cs_, :], p[:].rearrange("p (c t) -> p c t", c=4), h32[:, cs_, :], ALU.add, [p, h32], [h1])
            finish_layer(h1, t)

        def seq(t, sl):
            chain(t, sl)
            post(t)

        pre(0, 0)
        for t in range(NT):
            streams = [k.record(lambda: seq(t, t % 2), range(0, 4))]
            if t + 1 < NT:
                streams.append(k.record(lambda: pre(t + 1, (t + 1) % 2), range(4, 8)))
            k.merge(streams)
        k.release(m0)

    def make_ple(li, nslots=1):
        Wg = k.sb([128, 8, 1024], BF16, "Wg")
        Wp = k.sb([128, 2, 1024], BF16, "Wp")

        def load():
            load_w(Wg, lambda kc, c0, cw: Wg[:, kc, c0:c0 + cw], lambda kc, c0, cw: W["ple_gate"][li, :, kc, c0:c0 + cw], 8, 1024)
            load_w(Wp, lambda kc, c0, cw: Wp[:, kc, c0:c0 + cw], lambda kc, c0, cw: W["ple_w"][li, :, kc, c0:c0 + cw], 2, 1024)

        B = []

        def alloc(shared=None):
            for sl in range(nslots):
                d_ = dict(sqb=k.sb([128, 8, T], BF16, "sqbP"), rstd=k.sb([128, T], F32, "rstdP"),
                          n2=k.sb([128, 8, T], BF16, "n2"), p32=k.sb([128, 2, T], F32, "p32"), pb=k.sb([128, 2, T], BF16, "pbP"))
                for nm in ("tmp", "sg", "ho"):
                    d_[nm] = shared[nm] if shared is not None else k.sb([128, 8, T], F32, nm + "P")
                B.append(d_)

        def fn(h1, t, slot=0):
            b_ = B[slot]
            sqb, rstd, tmp, n2, sg, p32, pb, ho = (b_[x] for x in ("sqb", "rstd", "tmp", "n2", "sg", "p32", "pb", "ho"))
            t0 = t * T
            k.dma("sp", p32[:], pT[li, :, :, t0:t0 + T], [pT], [p32])
            k.cp("pool", pb[:], p32[:], [p32], [pb])
            rmsnorm(h1, "ple_g%d" % li, n2[:], n2, tmp, sqb, rstd)
            for hf in range(2):
                cs_ = slice(hf * 4, hf * 4 + 4)
                p = k.ps()
                for j in range(4):
                    oc = hf * 4 + j
                    for kc in range(8):
                        k.mm(p[:, j * T:(j + 1) * T], Wg[:, kc, oc * 128:(oc + 1) * 128], n2[:, kc, :], kc == 0, kc == 7, [Wg, n2], [p])
                k.act(sg[:, cs_, :], p[:].rearrange("p (c t) -> p c t", c=4), AF.Sigmoid, [p], [sg])
                p2 = k.ps()
                for j in range(4):
                    oc = hf * 4 + j
                    for kc in range(2):
                        k.mm(p2[:, j * T:(j + 1) * T], Wp[:, kc, oc * 128:(oc + 1) * 128], pb[:, kc, :], kc == 0, kc == 1, [Wp, pb], [p2])
                k.tt("dve", sg[:, cs_, :], p2[:].rearrange("p (c t) -> p c t", c=4), sg[:, cs_, :], ALU.mult, [p2, sg], [sg])
                k.tt("pool", ho[:, cs_, :], sg[:, cs_, :], h1[:, cs_, :], ALU.add, [sg, h1], [ho])
            k.dma("pool", hT[:, :, t0:t0 + T], ho[:], [ho], [hT])
        return load, alloc, fn

    def run_pairs(tile_fn, n):
        for t0 in range(0, n, 2):
            streams = [k.record(lambda t=t: tile_fn(t, t - t0), range(4 * (t - t0), 4 * (t - t0) + 4)) for t in range(t0, min(n, t0 + 2))]
            k.merge(streams)

    KTs = dscr("KTs", [128, 8, S_LEN], BF16)
    Vtok = dscr("Vtok", [S_LEN, 1024], BF16)
    OGs = dscr("OGs", [128, 8, S_LEN], BF16)

    def kv_phase():
        m0 = k.mark()
        Wkv = k.sb([128, 8, 2048], BF16, "Wkv")
        with_stage(lambda: load_w(Wkv, lambda kc, c0, cw: Wkv[:, kc, c0:c0 + cw], lambda kc, c0, cw: W["w_kv"][:, kc, c0:c0 + cw], 8, 2048))
        B = [dict(h32=k.sb([128, 8, T], F32, "h32K"), sqb=k.sb([128, 8, T], BF16, "sqbK"), rstd=k.sb([128, T], F32, "rstdK"),
                  tmp=k.sb([128, 8, T], F32, "tmpK"), nb=k.sb([128, 8, T], BF16, "nbK"), kT=k.sb([128, 8, T], BF16, "kTK"),
                  vt=k.sb([128, 1024], BF16, "vtK")) for _ in range(2)]

        def tile(t, slot):
            b_ = B[slot]
            h32, sqb, rstd, tmp, nb_, kT, vt = (b_[x] for x in ("h32", "sqb", "rstd", "tmp", "nb", "kT", "vt"))
            t0 = t * T
            k.dma("sp", h32[:], hT[:, :, t0:t0 + T], [hT], [h32])
            rmsnorm(h32, "kv_ln_g", nb_[:], nb_, tmp, sqb, rstd)
            for hf in range(2):
                p = k.ps()
                for j in range(4):
                    oc = hf * 4 + j
                    for kc in range(8):
                        k.mm(p[:, j * T:(j + 1) * T], Wkv[:, kc, oc * 128:(oc + 1) * 128], nb_[:, kc, :], kc == 0, kc == 7, [Wkv, nb_], [p])
                k.cp("act", kT[:, hf * 4:hf * 4 + 4, :], p[:].rearrange("p (c t) -> p c t", c=4), [p], [kT])
            k.dma("pool", KTs[:, :, t0:t0 + T], kT[:], [kT], [KTs])
            for hf in range(2):
                p = k.ps()
                for kc in range(8):
                    k.mm(p[:, 0:512], nb_[:, kc, :], Wkv[:, kc, 1024 + hf * 512:1024 + (hf + 1) * 512], kc == 0, kc == 7, [Wkv, nb_], [p])
                k.cp("dve", vt[:, hf * 512:(hf + 1) * 512], p[:], [p], [vt])
            k.dma("pool", Vtok[t0:t0 + T, :], vt[:], [vt], [Vtok])
        run_pairs(tile, NT)
        k.release(m0)

    DIL = (1, 4, 16)

    def attn_layer(li):
        j_ = li - 2
        m0 = k.mark()
        xnT = k.sb([128, 8, S_LEN], BF16, "xnT")
        m1 = k.mark()
        B1 = [dict(h32=k.sb([128, 8, T], F32, "h32A"), sqb=k.sb([128, 8, T], BF16, "sqbA"), rstd=k.sb([128, T], F32, "rstdA"),
                   tmp=k.sb([128, 8, T], F32, "tmpA2")) for _ in range(2)]

        def tile1(t, slot):
            b_ = B1[slot]
            t0 = t * T
            k.dma("sp", b_["h32"][:], hT[:, :, t0:t0 + T], [hT], [b_["h32"]])
            rmsnorm(b_["h32"], "b_ln_g%d" % j_, xnT[:, :, t0:t0 + T], xnT, b_["tmp"], b_["sqb"], b_["rstd"])
        run_pairs(tile1, NT)
        k.release(m1)
        stage[:] = [k.sb([128, 1024], F32, "stageA%d" % i) for i in range(3)]
        am32 = k.sb([128, 8, 256], F32, "am32")
        k.dma("sp", am32[:], amask_d, [], [am32])
        amb = k.sb([128, 8, 256], BF16, "amb")
        k.cp("dve", amb[:], am32[:], [am32], [amb])
        onesb = k.sb([128, 128], BF16, "onesb")
        k.memset("dve", onesb[:], 1.0, [onesb])
        Wh = k.sb([128, 8, 4, 128], BF16, "Wh")
        Kp = [k.sb([128, S_LEN], BF16, "Kp%d" % g) for g in range(3)]
        Qp = [k.sb([128, S_LEN], BF16, "Qp%d" % g) for g in range(3)]
        sgh = k.sb([128, S_LEN], BF16, "sgh")
        Vp = k.sb([128, 32, 128], BF16, "Vp")
        Oacc = k.sb([128, 2, S_LEN], F32, "Oacc")
        Pb = [k.sb([128, 256], BF16, "Pb%d" % i) for i in range(4)]
        ogh = Kp[2]
        qscale = float(128 ** -0.5)
        for h in range(8):
            for g in range(4):
                st = stage[stage_rr[0] % 3]; stage_rr[0] += 1
                col = (g * 1024 if g < 3 else 3072) + h * 128
                k.dma("sp", st[:].rearrange("p (c n) -> p c n", c=8), W["b_w_in"][j_, :, :, col:col + 128], [], [st])
                k.cp(k.alt(("act", "pool")), Wh[:, :, g, :], st[:].rearrange("p (c n) -> p c n", c=8), [st], [Wh])
            k.dma("sp", Kp[0][:], KTs[:, h, :], [KTs], [Kp[0]])
            for g in (1, 2):
                d = DIL[g]
                k.cp("pool", Kp[g][:].rearrange("p (r n) -> p r n", r=d), Kp[0][:].rearrange("p (n r) -> p r n", r=d), [Kp[0]], [Kp[g]])
            for tt in range(8):
                ts_ = slice(tt * 512, (tt + 1) * 512)
                for g in range(3):
                    d = DIL[g]
                    p = k.ps()
                    for kc in range(8):
                        k.mm(p[:, 0:512], Wh[:, kc, g, :], xnT[:, kc, ts_], kc == 0, kc == 7, [Wh, xnT], [p])
                    k.act(Qp[g][:].rearrange("p (r n) -> p r n", r=d)[:, :, tt * 512 // d:(tt + 1) * 512 // d],
                          p[:, 0:512].rearrange("p (n r) -> p r n", r=d), AF.Copy, [p], [Qp[g]], scale=qscale)
                p = k.ps()
                for kc in range(8):
                    k.mm(p[:, 0:512], Wh[:, kc, 3, :], xnT[:, kc, ts_], kc == 0, kc == 7, [Wh, xnT], [p])
                k.act(sgh[:, ts_], p[:, 0:512], AF.Silu, [p], [sgh])
            blocks = []
            for g in range(3):
                d = DIL[g]
                nb = 32 // d
                for r in range(d):
                    for c in range(nb):
                        blocks.append((g, d, nb, r, c))
            state = {"g": -1}

            def qk_stage(i):
                g, d, nb, r, c = blocks[i]
                blk = r * nb + c
                cols = slice(blk * 128, (blk + 1) * 128)
                pcols = slice((blk - 1) * 128, blk * 128)
                P_ = Pb[i % 4]
                ps_s = k.ps()
                k.mm(ps_s[:, 128:256], Kp[g][:, cols], Qp[g][:, cols], True, False, [Kp[g], Qp[g]], [ps_s])
                k.mm(ps_s[:, 128:256], identb[:], amb[:, h, 128:256], False, True, [identb, amb], [ps_s])
                if c > 0:
                    k.mm(ps_s[:, 0:128], Kp[g][:, pcols], Qp[g][:, cols], True, False, [Kp[g], Qp[g]], [ps_s])
                    k.mm(ps_s[:, 0:128], identb[:], amb[:, h, 0:128], False, True, [identb, amb], [ps_s])
                lo = 0 if c > 0 else 128
                k.act(P_[:, lo:256], ps_s[:, lo:256], AF.Exp, [ps_s], [P_])

            def pv_stage(i):
                g, d, nb, r, c = blocks[i]
                blk = r * nb + c
                P_ = Pb[i % 4]
                if state["g"] != g:
                    for r2 in range(d):
                        src = Vtok.h.rearrange("(c j r) f -> r j c f", j=128, r=d)[r2, :, :, h * 128:(h + 1) * 128]
                        k.dma("sp", Vp[:, r2 * nb:(r2 + 1) * nb, :], src, [Vtok], [Vp])
                    state["g"] = g
                Ov = Oacc[:].rearrange("p a (n r) -> p a r n", r=d)
                ps_o = k.ps()
                if c > 0:
                    k.mm(ps_o[:, 0:128], Vp[:, blk - 1, :], P_[:, 0:128], True, False, [Vp, P_], [ps_o])
                k.mm(ps_o[:, 0:128], Vp[:, blk, :], P_[:, 128:256], c == 0, True, [Vp, P_], [ps_o])
                if c > 0:
                    k.mm(ps_o[:, 128:256], onesb[:], P_[:, 0:128], True, False, [onesb, P_], [ps_o])
                k.mm(ps_o[:, 128:256], onesb[:], P_[:, 128:256], c == 0, True, [onesb, P_], [ps_o])
                dst = Ov[:, :, r, c * 128:(c + 1) * 128]
                srcp = ps_o[:, 0:256].rearrange("p (a i) -> p a i", a=2)
                if g == 0:
                    k.cp("dve", dst, srcp, [ps_o], [Oacc])
                else:
                    k.tt("dve", dst, srcp, dst, ALU.add, [ps_o, Oacc], [Oacc])

            LOOK = 2
            for i in range(len(blocks) + LOOK):
                if i < len(blocks):
                    qk_stage(i)
                if i - LOOK >= 0:
                    pv_stage(i - LOOK)
            k.recip(Oacc[:, 1, :], Oacc[:, 1, :], [Oacc], [Oacc])
            k.tt("dve", Oacc[:, 0, :], Oacc[:, 0, :], Oacc[:, 1, :], ALU.mult, [Oacc], [Oacc])
            k.tt("pool", ogh[:], Oacc[:, 0, :], sgh[:], ALU.mult, [Oacc, sgh], [ogh])
            k.dma("pool", OGs[:, h, :], ogh[:], [ogh], [OGs])
        k.release(m0)
        stage[:] = []
        Wo = k.sb([128, 8, 1024], BF16, "WoA")
        ple_load, ple_alloc, finish_layer = make_ple(li, 2)

        def load_C():
            load_w(Wo, lambda kc, c0, cw: Wo[:, kc, c0:c0 + cw], lambda kc, c0, cw: W["b_w_o"][j_, :, kc, c0:c0 + cw], 8, 1024)
            ple_load()
        with_stage(load_C)
        ple_alloc()
        B3 = [dict(h32=k.sb([128, 8, T], F32, "h32C"), og=k.sb([128, 8, T], BF16, "ogC"), h1=k.sb([128, 8, T], F32, "h1C")) for _ in range(2)]

        def tile3(t, slot):
            b_ = B3[slot]
            h32, og, h1 = b_["h32"], b_["og"], b_["h1"]
            t0 = t * T
            k.dma("sp", h32[:], hT[:, :, t0:t0 + T], [hT], [h32])
            k.dma("sp", og[:], OGs[:, :, t0:t0 + T], [OGs], [og])
            for hf in range(2):
                cs_ = slice(hf * 4, hf * 4 + 4)
                p = k.ps()
                for j in range(4):
                    oc = hf * 4 + j
                    for kc in range(8):
                        k.mm(p[:, j * T:(j + 1) * T], Wo[:, kc, oc * 128:(oc + 1) * 128], og[:, kc, :], kc == 0, kc == 7, [Wo, og], [p])
                k.tt("dve", h1[:, cs_, :], p[:].rearrange("p (c t) -> p c t", c=4), h32[:, cs_, :], ALU.add, [p, h32], [h1])
            finish_layer(h1, t, slot)
        run_pairs(tile3, NT)
        k.release(m0)

    def final_phase(gname="final_ln_g"):
        m0 = k.mark()
        B = [dict(h32=k.sb([128, 8, T], F32, "h32F"), sqb=k.sb([128, 8, T], BF16, "sqbF"), rstd=k.sb([128, T], F32, "rstdF"),
                  tmp=k.sb([128, 8, T], F32, "tmpF"), o32=k.sb([128, 8, T], F32, "o32F")) for _ in range(2)]

        def tile(t, slot):
            b_ = B[slot]
            t0 = t * T
            k.dma("sp", b_["h32"][:], hT[:, :, t0:t0 + T], [hT], [b_["h32"]])
            rmsnorm(b_["h32"], gname, b_["o32"][:], b_["o32"], b_["tmp"], b_["sqb"], b_["rstd"])
            k.dma("pool", outT[:, :, t0:t0 + T], b_["o32"][:], [b_["o32"]], [outT])
        run_pairs(tile, NT)
        k.release(m0)

    def dump_h():
        m0 = k.mark()
        h32 = k.sb([128, 8, T], F32, "h32D")
        for t in range(NT):
            k.dma("sp", h32[:], hT[:, :, t * T:(t + 1) * T], [hT], [h32])
            k.dma("pool", outT[:, :, t * T:(t + 1) * T], h32[:], [h32], [outT])
        k.release(m0)

    rwkv_layer(0, xT)
    if isinstance(stop_after, str):
        pass
    elif stop_after == 0:
        dump_h()
    else:
        rwkv_layer(1, hT)
        if stop_after == 1:
            dump_h()
        else:
            kv_phase()
            attn_layer(2)
            if stop_after == 2:
                dump_h()
            else:
                attn_layer(3)
                if stop_after == 3:
                    dump_h()
                else:
                    final_phase()
    k.s.barrier()
    k.s.emit()
    return nc


def _fm(w):
    K_, N_ = w.shape
    return np.ascontiguousarray(w.reshape(K_ // 128, 128, N_).transpose(1, 0, 2))


def _vec(v):
    return np.asarray(v, np.float32).reshape(8, 128).T


def host_inputs(inp):
    f = lambda a: np.asarray(a, np.float32)
    common = {}
    common["a_w_rkvg"] = np.stack([np.stack([_fm(f(inp["a_w_rkvg"])[i, n]) for n in range(4)]) for i in range(2)])
    common["a_w1"] = np.stack([_fm(f(inp["a_w1"])[i]) for i in range(2)])
    common["a_w2"] = np.ascontiguousarray(f(inp["a_w2"]))
    common["a_a1"] = np.stack([_fm(f(inp["a_a1"])[i]) for i in range(2)])
    common["a_a2"] = np.ascontiguousarray(f(inp["a_a2"]))
    common["a_v1"] = np.stack([_fm(f(inp["a_v1"])[0])])
    common["a_v2"] = np.ascontiguousarray(f(inp["a_v2"]))
    common["a_w_o"] = np.stack([_fm(f(inp["a_w_o"])[i]) for i in range(2)])
    common["ple_w"] = np.stack([_fm(f(inp["ple_w"])[i]) for i in range(4)])
    common["ple_w_gate"] = np.stack([_fm(f(inp["ple_w_gate"])[i]) for i in range(4)])
    common["w_kv"] = _fm(f(inp["w_kv"]))
    common["b_w_in"] = np.stack([_fm(f(inp["b_w_in"])[i]) for i in range(2)])
    common["b_w_o"] = np.stack([_fm(f(inp["b_w_o"])[i]) for i in range(2)])
    vecs = {}
    for i in range(2):
        vecs["a_ln_g%d" % i] = inp["a_ln_g"][i]
        for n in range(6):
            vecs["a_mu%d_%d" % (i, n)] = inp["a_mu"][i, n]
        for s in ("a_w0", "a_a0", "a_k_k", "a_k_a", "a_gn_g", "a_gn_b", "b_ln_g"):
            vecs[s + str(i)] = inp[s][i]
        vecs["a_r_k%d" % i] = np.asarray(inp["a_r_k"][i]).reshape(-1)
    vecs["a_v0"] = inp["a_v0"][0]
    vecs["kv_ln_g"] = inp["kv_ln_g"]
    vecs["final_ln_g"] = inp["final_ln_g"]
    for i in range(4):
        vecs["ple_g%d" % i] = inp["ple_gate_ln_g"][i]
    common["pvec"] = np.ascontiguousarray(np.concatenate([_vec(vecs[n]) for n in VEC_NAMES], axis=1))
    ii = np.arange(128)
    cm = np.zeros((128, 4, 128), np.float32)
    cm[:, 0, :] = np.eye(128)
    cm[:, 1, :] = (ii[:, None] < ii[None, :])
    cm[:, 2, :] = (ii[:, None] <= ii[None, :])
    cm[:, 3, :] = (ii[:, None] > ii[None, :])
    common["cmask"] = cm
    am = np.zeros((128, 8, 256), np.float32)
    for h in range(8):
        slope = 2.0 ** (-(h + 1))
        dprev = 128 + ii[None, :] - ii[:, None]
        dcur = ii[None, :] - ii[:, None]
        am[:, h, 0:128] = np.where((dprev >= 0) & (dprev <= 128), -slope * dprev, -30000.0)
        am[:, h, 128:256] = np.where((dcur >= 0) & (dcur <= 128), -slope * dcur, -30000.0)
    common["amask"] = am
    x = f(inp["x"])
    p = f(inp["p"])
    in_maps = []
    for b in range(8):
        m = dict(common)
        m["xT"] = np.ascontiguousarray(x[b].T.reshape(8, 128, S_LEN).transpose(1, 0, 2))
        m["pT"] = np.ascontiguousarray(p[:, b].transpose(0, 2, 1).reshape(4, 2, 128, S_LEN).transpose(0, 2, 1, 3))
        in_maps.append(m)
    return in_maps


_NC_CACHE = {}


def run(inp, stop_after=None, cores=8):
    in_maps = host_inputs(inp)[:cores]
    if stop_after not in _NC_CACHE:
        _NC_CACHE[stop_after] = build_program(stop_after)
    nc = _NC_CACHE[stop_after]
    res = run_bass_kernel_spmd(nc, in_maps, core_ids=list(range(cores)))
    outs = []
    for r in res.results:
        o = r["outT"]
        outs.append(o.transpose(1, 0, 2).reshape(D, S_LEN).T)
    return np.ascontiguousarray(np.stack(outs)).astype(np.float32)


def kernel(**inputs):
    return run(inputs)
```
